# Optimizing a Trainium2 kernel written in Bass

```python
import math
import jax, jax.numpy as jnp
from jax import lax
import numpy as np

D_MODEL = 2048
BATCH = 2
SEQ = 4096
DEPTH = 1

HEAD_DIM = 64
N_Q_HEADS = 16
N_KV_HEADS = 2
Q_PER_KV = N_Q_HEADS // N_KV_HEADS
ATTN_WIDTH = N_Q_HEADS * HEAD_DIM
KV_WIDTH = N_KV_HEADS * HEAD_DIM
WINDOW = 128
BLOCK = 128
NEG_INF = -1e30
SSM_WIDTH = D_MODEL // 2
SSM_GROUP = 16
N_SSM_GROUPS = SSM_WIDTH // SSM_GROUP
SSM_STATE = 64
DT_MIN = 1e-3
DT_MAX = 1e-1
D_FF = 5632
CONV_WIDTH = 3
N_BRANCHES = 2
IN_WIDTH = ATTN_WIDTH + 2 * KV_WIDTH + SSM_WIDTH + N_BRANCHES * D_MODEL
RMS_EPS = 1e-6

kernel_name = 'hybrid_swa_s5_convffn_adaln'


def rmsnorm(x, g):
    xf = x.astype(jnp.float32)
    xf = xf * lax.rsqrt(jnp.mean(xf * xf, axis=-1, keepdims=True) + RMS_EPS)
    return xf.astype(x.dtype) * g


def sliding_window_attention(q, k, v, sinks):
    b, l = q.shape[0], q.shape[1]
    nb = l // BLOCK
    qb = q.reshape(b, nb, BLOCK, N_KV_HEADS, Q_PER_KV, HEAD_DIM)
    pad = ((0, 0), (BLOCK, 0), (0, 0), (0, 0))
    kp = jnp.pad(k, pad).reshape(b, nb + 1, BLOCK, N_KV_HEADS, HEAD_DIM)
    vp = jnp.pad(v, pad).reshape(b, nb + 1, BLOCK, N_KV_HEADS, HEAD_DIM)
    kw = jnp.concatenate([kp[:, :-1], kp[:, 1:]], axis=2)
    vw = jnp.concatenate([vp[:, :-1], vp[:, 1:]], axis=2)
    s = jnp.einsum('bnqhgd,bnkhd->bnhgqk', qb, kw).astype(jnp.float32) * (HEAD_DIM ** -0.5)
    qi = jnp.arange(BLOCK)[:, None]
    kj = jnp.arange(2 * BLOCK)[None, :]
    rel = qi + BLOCK - kj
    band = (rel >= 0) & (rel < WINDOW)
    key_pos = jnp.arange(nb)[:, None, None] * BLOCK - BLOCK + kj[None]
    mask = band[None] & (key_pos >= 0)
    s = jnp.where(mask[None, :, None, None], s, NEG_INF)
    sink = sinks.astype(jnp.float32).reshape(N_KV_HEADS, Q_PER_KV)[None, None, :, :, None, None]
    m = jnp.maximum(jnp.max(s, axis=-1, keepdims=True), sink)
    p = jnp.exp(s - m)
    p = p / (jnp.sum(p, axis=-1, keepdims=True) + jnp.exp(sink - m))
    o = jnp.einsum('bnhgqk,bnkhd->bnqhgd', p.astype(v.dtype), vw)
    return o.reshape(b, l, ATTN_WIDTH)


def s5_layer(u, a_re, a_im, log_dt, b_re, b_im, c_re, c_im, d_skip):
    bsz, l = u.shape[0], u.shape[1]
    f32 = jnp.float32
    ug = u.reshape(bsz, l, N_SSM_GROUPS, SSM_GROUP).astype(f32)
    dt = jnp.exp(log_dt.astype(f32))[:, None]
    ar, ai = a_re.astype(f32), a_im.astype(f32)
    mag = jnp.exp(ar * dt)
    lr, li = mag * jnp.cos(ai * dt), mag * jnp.sin(ai * dt)
    den = ar * ar + ai * ai
    zr = ((lr - 1.0) * ar + li * ai) / den
    zi = (li * ar - (lr - 1.0) * ai) / den
    br, bi = b_re.astype(f32), b_im.astype(f32)
    bbar_r = zr[:, :, None] * br - zi[:, :, None] * bi
    bbar_i = zr[:, :, None] * bi + zi[:, :, None] * br
    xr = jnp.einsum('blgp,gnp->blgn', ug, bbar_r)
    xi = jnp.einsum('blgp,gnp->blgn', ug, bbar_i)
    seq_r = jnp.broadcast_to(lr[None, None], (1, l, N_SSM_GROUPS, SSM_STATE))
    seq_i = jnp.broadcast_to(li[None, None], (1, l, N_SSM_GROUPS, SSM_STATE))

    def combine(e1, e2):
        a1r, a1i, b1r, b1i = e1
        a2r, a2i, b2r, b2i = e2
        return (a1r * a2r - a1i * a2i,
                a1r * a2i + a1i * a2r,
                a2r * b1r - a2i * b1i + b2r,
                a2r * b1i + a2i * b1r + b2i)

    _, _, hr, hi = lax.associative_scan(combine, (seq_r, seq_i, xr, xi), axis=1)
    y = (jnp.einsum('blgn,gpn->blgp', hr, c_re.astype(f32))
         - jnp.einsum('blgn,gpn->blgp', hi, c_im.astype(f32))
         + d_skip.astype(f32) * ug)
    return y.reshape(bsz, l, SSM_WIDTH).astype(u.dtype)


def conv_ffn(h, w_up, conv_w, conv_b, w_down):
    up = h @ w_up
    gate, val = jnp.split(up, 2, axis=-1)
    gate = lax.conv_general_dilated(
        gate, conv_w[:, None, :], window_strides=(1,), padding=((CONV_WIDTH - 1, 0),),
        dimension_numbers=('NWC', 'WIO', 'NWC'), feature_group_count=D_FF) + conv_b
    return (jax.nn.silu(gate) * val) @ w_down


def setup_inputs(seed: int = 0) -> dict:
    key = jax.random.key(seed)
    ks = jax.random.split(key, 24)
    f32 = jnp.float32

    def nrm(k, shape, scale):
        return jax.random.normal(k, shape, f32) * scale

    G, N, P = N_SSM_GROUPS, SSM_STATE, SSM_GROUP
    a_im0 = jnp.pi * jnp.arange(N, dtype=f32)
    return {
        'x': nrm(ks[0], (BATCH, SEQ, D_MODEL), 1.0),
        'c': nrm(ks[1], (BATCH, D_MODEL), 1.0),
        'ada_w': nrm(ks[2], (DEPTH, D_MODEL, 6 * D_MODEL), D_MODEL ** -0.5),
        'ada_b': nrm(ks[3], (DEPTH, 6 * D_MODEL), 0.01),
        'norm_mix_g': 1.0 + nrm(ks[4], (DEPTH, D_MODEL), 0.01),
        'w_in': nrm(ks[5], (DEPTH, D_MODEL, IN_WIDTH), D_MODEL ** -0.5),
        'attn_sinks': nrm(ks[6], (DEPTH, N_Q_HEADS), 0.5),
        'w_attn_proj': nrm(ks[7], (DEPTH, ATTN_WIDTH, D_MODEL), ATTN_WIDTH ** -0.5),
        'ssm_a_re': -0.5 + nrm(ks[8], (DEPTH, G, N), 0.01),
        'ssm_a_im': a_im0 + nrm(ks[9], (DEPTH, G, N), 0.01),
        'ssm_log_dt': jax.random.uniform(ks[10], (DEPTH, G), f32, math.log(DT_MIN), math.log(DT_MAX)),
        'ssm_b_re': nrm(ks[11], (DEPTH, G, N, P), (2 * P) ** -0.5),
        'ssm_b_im': nrm(ks[12], (DEPTH, G, N, P), (2 * P) ** -0.5),
        'ssm_c_re': nrm(ks[13], (DEPTH, G, P, N), N ** -0.5),
        'ssm_c_im': nrm(ks[14], (DEPTH, G, P, N), N ** -0.5),
        'ssm_d': nrm(ks[15], (DEPTH, G, P), 1.0),
        'w_ssm_glu': nrm(ks[16], (DEPTH, SSM_WIDTH, 2 * D_MODEL), SSM_WIDTH ** -0.5),
        'w_out': nrm(ks[17], (DEPTH, D_MODEL, D_MODEL), D_MODEL ** -0.5),
        'norm_ffn_g': 1.0 + nrm(ks[18], (DEPTH, D_MODEL), 0.01),
        'w_ffn_up': nrm(ks[19], (DEPTH, D_MODEL, 2 * D_FF), D_MODEL ** -0.5),
        'ffn_conv_w': nrm(ks[20], (DEPTH, CONV_WIDTH, D_FF), CONV_WIDTH ** -0.5),
        'ffn_conv_b': nrm(ks[21], (DEPTH, D_FF), 0.01),
        'w_ffn_down': nrm(ks[22], (DEPTH, D_FF, D_MODEL), D_FF ** -0.5),
        'final_g': 1.0 + nrm(ks[23], (D_MODEL,), 0.01),
    }


def reference(x, c, ada_w, ada_b, norm_mix_g, w_in, attn_sinks, w_attn_proj,
              ssm_a_re, ssm_a_im, ssm_log_dt, ssm_b_re, ssm_b_im, ssm_c_re, ssm_c_im, ssm_d,
              w_ssm_glu, w_out, norm_ffn_g, w_ffn_up, ffn_conv_w, ffn_conv_b, w_ffn_down, final_g):
    bsz, l = x.shape[0], x.shape[1]
    cond = jax.nn.silu(c)
    splits = [ATTN_WIDTH, ATTN_WIDTH + KV_WIDTH, ATTN_WIDTH + 2 * KV_WIDTH,
              ATTN_WIDTH + 2 * KV_WIDTH + SSM_WIDTH, ATTN_WIDTH + 2 * KV_WIDTH + SSM_WIDTH + D_MODEL]
    for i in range(DEPTH):
        mod = (cond @ ada_w[i] + ada_b[i])[:, None, :]
        sh1, sc1, g1, sh2, sc2, g2 = jnp.split(mod, 6, axis=-1)
        h = rmsnorm(x, norm_mix_g[i]) * (1.0 + sc1) + sh1
        proj = h @ w_in[i]
        q, k, v, u, g_attn, g_ssm = jnp.split(proj, splits, axis=-1)
        q = q.reshape(bsz, l, N_Q_HEADS, HEAD_DIM)
        k = k.reshape(bsz, l, N_KV_HEADS, HEAD_DIM)
        v = v.reshape(bsz, l, N_KV_HEADS, HEAD_DIM)
        attn = sliding_window_attention(q, k, v, attn_sinks[i]) @ w_attn_proj[i]
        y = s5_layer(u, ssm_a_re[i], ssm_a_im[i], ssm_log_dt[i], ssm_b_re[i], ssm_b_im[i],
                     ssm_c_re[i], ssm_c_im[i], ssm_d[i])
        glu_a, glu_b = jnp.split(jax.nn.gelu(y) @ w_ssm_glu[i], 2, axis=-1)
        ssm = glu_a * jax.nn.sigmoid(glu_b)
        mixed = jax.nn.sigmoid(g_attn) * attn + jax.nn.sigmoid(g_ssm) * ssm
        x = x + g1 * (mixed @ w_out[i])
        h = rmsnorm(x, norm_ffn_g[i]) * (1.0 + sc2) + sh2
        x = x + g2 * conv_ffn(h, w_ffn_up[i], ffn_conv_w[i], ffn_conv_b[i], w_ffn_down[i])
    return rmsnorm(x, final_g)
```

```python
import os
import numpy as np
from contextlib import ExitStack
import concourse.bass as bass
import concourse.mybir as mybir
from concourse.bass_utils import run_bass_kernel_spmd

F32 = mybir.dt.float32
BF16 = mybir.dt.bfloat16
ALU = mybir.AluOpType
ACTF = mybir.ActivationFunctionType

D = 2048
KT = 16
NW = 32
TWO_PI = float(2 * np.pi)
MAGIC = 12582912.0
NEG = -30000.0
ENGS = ("pe", "dve", "act", "pool", "sp")


class Op:
    __slots__ = ("eng", "fn", "reads", "writes", "chan", "idx", "deps", "tick", "sig")

    def __init__(self, eng, fn, reads, writes, chan):
        self.eng, self.fn, self.reads, self.writes, self.chan = eng, fn, reads, writes, chan
        self.deps = {}
        self.tick = None
        self.sig = False


class Prog:
    def __init__(self):
        self.ops = []
        self.nchan = 0
        self.barriers = []

    def new_chan(self):
        self.nchan += 1
        return self.nchan - 1

    def op(self, eng, fn, reads=(), writes=(), chan=None):
        o = Op(eng, fn, tuple(reads), tuple(writes), chan)
        o.idx = len(self.ops)
        self.ops.append(o)
        return o

    def barrier(self):
        self.barriers.append(len(self.ops))

    def analyze(self):
        last_w, readers, last_sid, bar_last = {}, {}, {}, {}
        bars = set(self.barriers)
        for o in self.ops:
            if o.idx in bars:
                bar_last = dict(last_sid)
            cand = []
            for k in o.reads:
                w = last_w.get(k)
                if w is not None:
                    cand.append((w, "raw"))
            for k in o.writes:
                w = last_w.get(k)
                if w is not None:
                    cand.append((w, "waw"))
                for r in readers.get(k, ()):
                    cand.append((r, "war"))
            for p in bar_last.values():
                cand.append((p, "bar"))
            for (p, kind) in cand:
                if p is o:
                    continue
                if p.chan is None and o.chan is None and p.eng == o.eng:
                    if o.eng == "pe":
                        continue
                sid = ("c", p.chan) if p.chan is not None else ("e", p.eng)
                cur = o.deps.get(sid)
                if cur is None or cur.idx < p.idx:
                    o.deps[sid] = p
            for k in o.reads:
                readers.setdefault(k, []).append(o)
            for k in o.writes:
                last_w[k] = o
                readers[k] = []
            last_sid[("c", o.chan) if o.chan is not None else ("e", o.eng)] = o
        for o in self.ops:
            for p in o.deps.values():
                p.sig = True
        ticks = {}
        for o in self.ops:
            if o.chan is not None:
                sid = ("c", o.chan)
                ticks[sid] = ticks.get(sid, 0) + 16
                o.tick = ticks[sid]
                o.sig = True
            elif o.sig:
                sid = ("e", o.eng)
                ticks[sid] = ticks.get(sid, 0) + 1
                o.tick = ticks[sid]
        self.final_ticks = ticks

    def emit(self, nc, st):
        self.analyze()
        sems = {}
        for e in ENGS:
            sems[("e", e)] = st.enter_context(nc.semaphore("s_" + e))
        for c in range(self.nchan):
            sems[("c", c)] = st.enter_context(nc.semaphore("c_%d" % c))
        block = st.enter_context(nc.Block())
        per_eng = {e: [o for o in self.ops if o.eng == e] for e in ENGS}
        final_ticks = self.final_ticks

        def run(eng_name, eng):
            known = {}
            for o in per_eng[eng_name]:
                for sid, p in o.deps.items():
                    if known.get(sid, 0) >= p.tick:
                        continue
                    eng.wait_ge(sems[sid], p.tick)
                    known[sid] = p.tick
                ins = o.fn(eng)
                if o.sig:
                    sid = ("c", o.chan) if o.chan is not None else ("e", o.eng)
                    ins.then_inc(sems[sid], 16 if o.chan is not None else 1)
            if eng_name == "sp":
                for sid, t in final_ticks.items():
                    eng.wait_ge(sems[sid], t)

        @block.tensor
        def _(eng):
            run("pe", eng)

        @block.vector
        def _(eng):
            run("dve", eng)

        @block.scalar
        def _(eng):
            run("act", eng)

        @block.gpsimd
        def _(eng):
            run("pool", eng)

        @block.sync
        def _(eng):
            run("sp", eng)


class Arena:
    def __init__(self, ap, nbytes):
        self.ap, self.nbytes, self.off, self.peak = ap, nbytes, 0, 0

    def alloc(self, shape, dtype=F32):
        n = int(np.prod(shape))
        sz = 4 if dtype == F32 else 2
        b = (n * sz + 31) // 32 * 32
        assert self.off + b <= self.nbytes, ("SBUF arena overflow", self.off, b, self.nbytes)
        v = self.ap[:, self.off // 4:(self.off + b) // 4]
        if dtype != F32:
            v = v.bitcast(dtype)
        v = v[:, 0:n]
        self.off += b
        self.peak = max(self.peak, self.off)
        if len(shape) == 2:
            v = v.rearrange("p (a b) -> p a b", b=shape[1])
        elif len(shape) == 3:
            v = v.rearrange("p (a b c) -> p a b c", b=shape[1], c=shape[2])
        elif len(shape) == 4:
            v = v.rearrange("p (a b c d) -> p a b c d", b=shape[1], c=shape[2], d=shape[3])
        return v

    def alloc_top(self, shape, dtype=F32):
        n = int(np.prod(shape))
        sz = 4 if dtype == F32 else 2
        b = (n * sz + 31) // 32 * 32
        assert self.off + b <= self.nbytes, ("SBUF arena overflow(top)", self.off, b, self.nbytes)
        self.nbytes -= b
        v = self.ap[:, self.nbytes // 4:(self.nbytes + b) // 4]
        if dtype != F32:
            v = v.bitcast(dtype)
        v = v[:, 0:n]
        if len(shape) == 2:
            v = v.rearrange("p (a b) -> p a b", b=shape[1])
        return v, b

    def free_top(self, b):
        self.nbytes += b

    def mark(self):
        return self.off

    def reset(self, m):
        self.off = m


def f_dma(out, in_):
    return lambda e: e.dma_start(out=out, in_=in_)


def f_mm(out, lhsT, rhs, start=True, stop=True, sgc=False):
    if sgc:
        return lambda e: e.matmul(out, lhsT, rhs, start=start, stop=stop, skip_group_check=True)
    return lambda e: e.matmul(out, lhsT, rhs, start=start, stop=stop)


def f_tr(out, in_, ident):
    return lambda e: e.transpose(out, in_, ident)


def f_act(out, in_, func, bias=None, scale=None, accum_out=None):
    kw = {}
    if bias is not None:
        kw["bias"] = bias
    if scale is not None:
        kw["scale"] = scale
    if accum_out is not None:
        kw["accum_out"] = accum_out
    return lambda e: e.activation(out, in_, func, **kw)


def f_ts(out, in0, s1, s2, op0, op1=None):
    if op1 is None:
        return lambda e: e.tensor_scalar(out, in0, s1, None, op0)
    return lambda e: e.tensor_scalar(out, in0, s1, s2, op0, op1)


def f_tt(out, in0, in1, op):
    return lambda e: e.tensor_tensor(out, in0, in1, op)


def f_stt(out, in0, scalar, in1, op0, op1):
    return lambda e: e.scalar_tensor_tensor(out, in0, scalar, in1, op0, op1)


def f_copy(out, in_):
    return lambda e: e.tensor_copy(out, in_)


def f_memset(ap, v):
    return lambda e: e.memset(ap, v)


def f_recip(out, in_):
    return lambda e: e.reciprocal(out, in_)


def f_asel(out, in_, cmp, fill, base, pattern, cm):
    return lambda e: e.affine_select(out=out, in_=in_, compare_op=cmp, fill=fill, base=base,
                                     pattern=pattern, channel_multiplier=cm)


class Builder:
    def __init__(self, debug=(), stop_after=None):
        self.nc = nc = bass.Bass("TRN2", target_bir_lowering=False)
        self.P = Prog()
        self.debug = set(debug)
        self.stop_after = stop_after
        self.dbg = {}
        self.st = ExitStack()
        self.in_names = []
        d = self.dram_in
        self.xw = d("xw", [4096, D])
        self.cb = d("cb", [D])
        self.cstd = d("cst", [128, 64])
        self.ada_w = d("ada_w", [D, 6 * D])
        self.ada_b = d("ada_b", [6 * D])
        self.g_mix = d("g_mix", [D])
        self.w_in = d("w_in", [D, 6400])
        self.sinks = d("sinks", [16])
        self.w_ap = d("w_ap", [1024, D])
        self.a_re = d("a_re", [64, 64])
        self.a_im = d("a_im", [64, 64])
        self.log_dt = d("log_dt", [64])
        self.b_re = d("b_re", [64, 64, 16])
        self.b_im = d("b_im", [64, 64, 16])
        self.c_re = d("c_re", [64, 16, 64])
        self.c_im = d("c_im", [64, 16, 64])
        self.ssm_d = d("ssm_d", [64, 16])
        self.w_glu = d("w_glu", [1024, 2 * D])
        self.w_out = d("w_out", [D, D])
        self.g_ffn = d("g_ffn", [D])
        self.w_up = d("w_up", [D, 11264])
        self.conv_w = d("conv_w", [3, 5632])
        self.conv_b = d("conv_b", [5632])
        self.w_down = d("w_down", [5632, D])
        self.final_g = d("final_g", [D])
        self.out = nc.dram_tensor("out", [1024, D], F32, kind="ExternalOutput").ap()
        self.Wd = nc.dram_tensor("Wd", [64, 8, 2, 16, 64], BF16, kind="Internal").ap()
        self.abd = nc.dram_tensor("abd", [2, 64, 64], F32, kind="Internal").ap()
        self.rowd = nc.dram_tensor("rowd", [6, 64, 64], F32, kind="Internal").ap()
        W = 212000 // 4
        sb = self.st.enter_context(nc.sbuf_tensor("arena", [128, W], F32))
        self.A = Arena(sb[:], W * 4)
        self.ps = [self.st.enter_context(nc.psum_tensor("psb%d" % i, [128, 512], F32))[:] for i in range(8)]
        self.chan_cache = {}

    def dram_in(self, name, shape):
        if self.stop_after in ("setup", "ssm_pre", "phase_u", "ssm") and name in ("w_ap", "w_glu", "w_out", "w_up", "w_down"):
            return None
        self.in_names.append(name)
        return self.nc.dram_tensor(name, shape, F32, kind="ExternalInput").ap()

    def chan(self, name):
        if name not in self.chan_cache:
            self.chan_cache[name] = self.P.new_chan()
        return self.chan_cache[name]

    def dump(self, name, ap, shape, reads):
        if name not in self.debug:
            return
        dt = ap.dtype
        t = self.nc.dram_tensor("dbg_" + name, list(shape), dt, kind="ExternalOutput").ap()
        self.dbg[name] = "dbg_" + name
        self.P.op("sp", f_dma(t, ap), reads=reads, chan=self.chan("dbg"))

    def setup(self):
        A, P = self.A, self.P
        cst = self.cst = A.alloc((64,))
        P.op("sp", f_dma(cst, self.cstd), writes=["cst"], chan=self.chan("cst"))
        identF = self.identF = A.alloc((128,))
        identB = self.identB = A.alloc((128,), BF16)
        onesF = self.onesF = A.alloc((128,))
        onesB = self.onesB = A.alloc((128,), BF16)
        Ls = self.Lstrict = A.alloc((128,))
        P.op("dve", f_memset(identF, 0.0), writes=["identF"])
        P.op("pool", f_asel(identF, identF, ALU.not_equal, 1.0, 0, [[-1, 128]], 1), reads=["identF"], writes=["identF"])
        P.op("dve", f_copy(identB, identF), reads=["identF"], writes=["identB"])
        P.op("dve", f_memset(onesF, 1.0), writes=["onesF"])
        P.op("dve", f_memset(onesB, 1.0), writes=["onesB"])
        P.op("pool", f_asel(Ls, onesF, ALU.is_gt, 0.0, 0, [[1, 128]], -1), reads=["onesF"], writes=["Ls"])
        self.maskcat = A.alloc((256,), BF16)
        self.mask01 = A.alloc((256,), BF16)
        self.diag = A.alloc((2, 128))
        self.onespad = A.alloc((2, 128), BF16)
        P.op("dve", f_memset(self.onespad, 0.0), writes=["onespad"])
        P.op("dve", f_memset(self.onespad[:, 0, 0:64], 1.0), reads=["onespad"], writes=["onespad"])
        P.op("dve", f_memset(self.onespad[:, 1, 64:128], 1.0), reads=["onespad"], writes=["onespad"])
        gm = self.gmask = A.alloc((8,))
        P.op("dve", f_memset(gm, 1.0), writes=["gmask"])
        P.op("pool", f_asel(gm, gm, ALU.is_ge, 0.0, 0, [[-16, 8]], 1), reads=["gmask"], writes=["gmask"])
        P.op("pool", f_asel(gm, gm, ALU.is_ge, 0.0, 15, [[16, 8]], -1), reads=["gmask"], writes=["gmask"])
        m0 = A.mark()
        rows = self.rows = A.alloc((128,))
        self.MOD = A.alloc((96,))
        self.vecs = A.alloc((4, 16))
        self.adab = A.alloc((96,))
        self.condb = A.alloc((16,), BF16)
        ps = self.ps

        def to_fp(src_rows_ap, nrows, dst, key, tag):
            P.op("sp", f_dma(rows[0:nrows, :], src_rows_ap), writes=["rows"], chan=self.chan("rows"))
            P.op("pe", f_tr(ps[0][:, 0:nrows], rows[0:nrows, :], identF[0:nrows, 0:nrows]), reads=["rows", "identF"], writes=["ps0"])
            P.op("dve", f_copy(dst, ps[0][:, 0:nrows]), reads=["ps0"], writes=[key])

        to_fp(self.cb.rearrange("(a b) -> a b", b=128), 16, self.vecs[:, 0, :], "vec0", "c")
        to_fp(self.g_mix.rearrange("(a b) -> a b", b=128), 16, self.vecs[:, 1, :], "vec1", "gm")
        to_fp(self.g_ffn.rearrange("(a b) -> a b", b=128), 16, self.vecs[:, 2, :], "vec2", "gf")
        to_fp(self.ada_b.rearrange("(a b) -> a b", b=128), 96, self.adab, "adab", "ab")
        P.op("act", f_act(self.condb, self.vecs[:, 0, :], ACTF.Silu), reads=["vec0"], writes=["condb"])
        self.sp = A.alloc((2, 16))
        self.res0_mark = A.mark()
        self.Wc = A.alloc((8, 8, 2, 64), BF16)
        self.Kc = A.alloc((8, 8, 16), BF16)
        self.WoutK = A.alloc((64, 9, 16), BF16)
        self.Dcol = A.alloc((8,))
        self.RT = A.alloc((8, 6, 4))
        self.wt_mark = A.mark()
        mk = A.alloc((256,))
        P.op("dve", f_memset(mk, 0.0), writes=["mk"])
        P.op("pool", f_asel(mk[:, 0:128], mk[:, 0:128], ALU.is_ge, NEG, 0, [[1, 128]], -1), reads=["mk"], writes=["mk"])
        P.op("pool", f_asel(mk[:, 128:256], mk[:, 128:256], ALU.is_gt, NEG, 0, [[-1, 128]], 1), reads=["mk"], writes=["mk"])
        P.op("dve", f_copy(self.maskcat, mk), reads=["mk"], writes=["maskcat"])
        m01 = A.alloc((256,))
        P.op("dve", f_memset(m01, 1.0), writes=["m01"])
        P.op("pool", f_asel(m01[:, 0:128], m01[:, 0:128], ALU.is_ge, 0.0, 0, [[1, 128]], -1), reads=["m01"], writes=["m01"])
        P.op("pool", f_asel(m01[:, 128:256], m01[:, 128:256], ALU.is_gt, 0.0, 0, [[-1, 128]], 1), reads=["m01"], writes=["m01"])
        P.op("dve", f_copy(self.mask01, m01), reads=["m01"], writes=["mask01"])
        cm = self.colmask = A.alloc((8, 128))
        P.op("dve", f_memset(cm, 1.0), writes=["colmask"])
        P.op("pool", f_asel(cm, cm, ALU.is_ge, 0.0, 0, [[-16, 8], [1, 128]], 0), reads=["colmask"], writes=["colmask"])
        P.op("pool", f_asel(cm, cm, ALU.is_ge, 0.0, 15, [[16, 8], [-1, 128]], 0), reads=["colmask"], writes=["colmask"])
        NS = 2
        wt = [A.alloc((16, 512), BF16) for _ in range(NS)]
        aw = self.ada_w.rearrange("(kt p) m -> p kt m", p=128)

        def load(blk):
            s = blk % NS
            P.op("pool", f_dma(wt[s], aw[:, :, blk * 512:(blk + 1) * 512]), writes=["adaw%d" % s], chan=self.chan("adaw%d" % s))

        for blk in range(NS):
            load(blk)

        rowb = [A.alloc((512,)) for _ in range(2)]
        one1 = self.onesF[0:1, 0:1]

        def mod_tiles(lo, hi):
            for blk in range(lo // 4, hi // 4):
                s = blk % NS
                for kt in range(KT):
                    P.op("pe", f_mm(ps[0][0:1, :], self.condb[:, kt:kt + 1], wt[s][:, kt, :], kt == 0, kt == KT - 1),
                         reads=["adaw%d" % s, "condb"], writes=["ps0"])
                rb = rowb[blk % 2]
                P.op("act", f_act(rb[0:1], ps[0][0:1, :], ACTF.Identity), reads=["ps0"], writes=["rowb%d" % (blk % 2)])
                for c in range(4):
                    mt = blk * 4 + c
                    P.op("pe", f_mm(ps[1][:, mt:mt + 1], rb[0:1, c * 128:(c + 1) * 128], one1, True, True, True),
                         reads=["rowb%d" % (blk % 2), "onesF"], writes=["ps1"])
                if blk + NS < 24:
                    load(blk + NS)

        self.mod_tiles = mod_tiles

    def setup_rest(self):
        self.P.barrier()

    def setup_b(self):
        P, ps = self.P, self.ps
        P.op("dve", f_tt(self.MOD, ps[1][:, 0:96], self.adab, ALU.add), reads=["ps1", "adab"], writes=["MOD"])
        P.op("dve", f_stt(self.sp[:, 0, :], self.MOD[:, 16:32], 1.0, self.vecs[:, 1, :], ALU.add, ALU.mult), reads=["MOD", "vec1"], writes=["sp"])
        P.op("dve", f_stt(self.sp[:, 1, :], self.MOD[:, 64:80], 1.0, self.vecs[:, 2, :], ALU.add, ALU.mult), reads=["MOD", "vec2", "sp"], writes=["sp"])
        self.dump("MOD", self.MOD, [128, 96], ["MOD"])
        self.dump("sp", self.sp, [128, 2, 16], ["sp"])

    def late_consts(self):
        A, P, ps = self.A, self.P, self.ps
        identF, onesF = self.identF, self.onesF
        self.gbc = A.alloc((2, D))
        self.fgbc = A.alloc((D,))
        diag = A.alloc((2, 128))
        for gi, c0 in ((0, 32), (1, 80)):
            for kt in range(KT):
                b = kt % 2
                P.op("dve", f_ts(diag[:, b, :], identF, self.MOD[:, c0 + kt:c0 + kt + 1], None, ALU.mult), reads=["identF", "MOD"], writes=["diag%d" % b])
                bank = 2 + kt // 4
                P.op("pe", f_mm(ps[bank][:, (kt % 4) * 128:(kt % 4 + 1) * 128], onesF, diag[:, b, :]), reads=["onesF", "diag%d" % b], writes=["ps%d" % bank])
            for q in range(4):
                P.op("act", f_act(self.gbc[:, gi, q * 512:(q + 1) * 512], ps[2 + q], ACTF.Identity), reads=["ps%d" % (2 + q)], writes=["gbc"])
        P.op("sp", f_dma(self.fgbc, self.final_g.partition_broadcast(128)), writes=["fgbc"], chan=self.chan("fgbc"))
        self.dump("gbc", self.gbc, [128, 2, D], ["gbc"])

    def wrap_sincos(self, eng, phi, temps, key, want):
        P = self.P
        ti = 0
        for (outap, kind, okey) in want:
            src, skey = phi, key
            if kind == "cos":
                psi, pk = temps[ti]; ti += 1
                P.op(eng, f_ts(psi, phi, float(np.pi / 2), None, ALU.add), reads=[key], writes=[pk])
                src, skey = psi, pk
            k1, kk = temps[ti]; ti += 1
            P.op(eng, f_ts(k1, src, 1.0 / TWO_PI, MAGIC, ALU.mult, ALU.add), reads=[skey], writes=[kk])
            P.op(eng, f_ts(k1, k1, -MAGIC, -TWO_PI, ALU.add, ALU.mult), reads=[kk], writes=[kk])
            P.op(eng, f_tt(k1, src, k1, ALU.add), reads=[skey, kk], writes=[kk])
            P.op(eng, f_ts(k1, k1, -3.1415925, 3.1415925, ALU.max, ALU.min), reads=[kk], writes=[kk])
            P.op("act", f_act(outap, k1, ACTF.Sin), reads=[kk], writes=[okey])

    def ssm_precompute(self):
        A, P, ps = self.A, self.P, self.ps
        identF, identB = self.identF, self.identB
        res_mark = self.wt_mark
        G = 64
        ar = A.alloc((64,)); ai = A.alloc((64,)); ldt = A.alloc((1,)); dt = A.alloc((1,))
        P.op("sp", f_dma(ar[0:G], self.a_re), writes=["ar"], chan=self.chan("ssmld1"))
        P.op("sp", f_dma(ai[0:G], self.a_im), writes=["ai"], chan=self.chan("ssmld2"))
        P.op("sp", f_dma(ldt[0:G], self.log_dt.rearrange("(g o) -> g o", o=1)), writes=["ldt"], chan=self.chan("ssmld3"))
        P.op("act", f_act(dt[0:G], ldt[0:G], ACTF.Exp), reads=["ldt"], writes=["dt"])
        ardt = A.alloc((64,)); aidt = A.alloc((64,))
        P.op("dve", f_ts(ardt[0:G], ar[0:G], dt[0:G], None, ALU.mult), reads=["ar", "dt"], writes=["ardt"])
        P.op("dve", f_ts(aidt[0:G], ai[0:G], dt[0:G], None, ALU.mult), reads=["ai", "dt"], writes=["aidt"])
        ks = [0, 1, 2, 3, 4, 5, 6, 7, 8, 504, 1024, 520]
        NK = len(ks)
        Lr = A.alloc((NK, 64)); Li = A.alloc((NK, 64)); mag = A.alloc((NK, 64)); ang = A.alloc((NK, 64))
        for i, k in enumerate(ks):
            P.op("act", f_act(mag[0:G, i, :], ardt[0:G], ACTF.Exp, scale=float(k)), reads=["ardt"], writes=["mag"])
            P.op("dve", f_ts(ang[0:G, i, :], aidt[0:G], float(k), None, ALU.mult), reads=["aidt"], writes=["ang"])
        sn = A.alloc((NK, 64)); cs = A.alloc((NK, 64))
        tmps = [(A.alloc((NK, 64))[0:G], "sctmp%d" % i) for i in range(3)]
        self.wrap_sincos("dve", ang[0:G], tmps, "ang", [(sn[0:G], "sin", "sn"), (cs[0:G], "cos", "cs")])
        P.op("dve", f_tt(Lr[0:G], mag[0:G], cs[0:G], ALU.mult), reads=["mag", "cs"], writes=["Lr"])
        P.op("dve", f_tt(Li[0:G], mag[0:G], sn[0:G], ALU.mult), reads=["mag", "sn"], writes=["Li"])
        den = A.alloc((64,)); t1 = A.alloc((64,)); t2 = A.alloc((64,)); lm1 = A.alloc((64,)); zr = A.alloc((64,)); zi = A.alloc((64,))
        P.op("dve", f_tt(den[0:G], ar[0:G], ar[0:G], ALU.mult), reads=["ar"], writes=["den"])
        P.op("dve", f_tt(t1[0:G], ai[0:G], ai[0:G], ALU.mult), reads=["ai"], writes=["t1"])
        P.op("dve", f_tt(den[0:G], den[0:G], t1[0:G], ALU.add), reads=["den", "t1"], writes=["den"])
        P.op("dve", f_recip(den[0:G], den[0:G]), reads=["den"], writes=["den"])
        P.op("dve", f_ts(lm1[0:G], Lr[0:G, 1, :], -1.0, None, ALU.add), reads=["Lr"], writes=["lm1"])
        P.op("dve", f_tt(t1[0:G], lm1[0:G], ar[0:G], ALU.mult), reads=["lm1", "ar"], writes=["t1"])
        P.op("dve", f_tt(t2[0:G], Li[0:G, 1, :], ai[0:G], ALU.mult), reads=["Li", "ai"], writes=["t2"])
        P.op("dve", f_tt(t1[0:G], t1[0:G], t2[0:G], ALU.add), reads=["t1", "t2"], writes=["t1"])
        P.op("dve", f_tt(zr[0:G], t1[0:G], den[0:G], ALU.mult), reads=["t1", "den"], writes=["zr"])
        P.op("dve", f_tt(t1[0:G], Li[0:G, 1, :], ar[0:G], ALU.mult), reads=["Li", "ar"], writes=["t1"])
        P.op("dve", f_tt(t2[0:G], lm1[0:G], ai[0:G], ALU.mult), reads=["lm1", "ai"], writes=["t2"])
        P.op("dve", f_tt(t1[0:G], t1[0:G], t2[0:G], ALU.subtract), reads=["t1", "t2"], writes=["t1"])
        P.op("dve", f_tt(zi[0:G], t1[0:G], den[0:G], ALU.mult), reads=["t1", "den"], writes=["zi"])
        br = A.alloc((64, 16)); bi = A.alloc((64, 16))
        P.op("sp", f_dma(br[0:G], self.b_re), writes=["br"], chan=self.chan("ssmld4"))
        P.op("sp", f_dma(bi[0:G], self.b_im), writes=["bi"], chan=self.chan("ssmld5"))
        brT = br[0:G].rearrange("g n q -> g q n"); biT = bi[0:G].rearrange("g n q -> g q n")
        Bb = A.alloc((2, 16, 64)); tq = A.alloc((16, 64)); tq2 = A.alloc((16, 64))

        def bc_n(v):
            return v.unsqueeze(1).to_broadcast([G, 16, 64])

        P.op("dve", f_tt(tq[0:G], brT, bc_n(zr[0:G]), ALU.mult), reads=["br", "zr"], writes=["tq"])
        P.op("dve", f_tt(tq2[0:G], biT, bc_n(zi[0:G]), ALU.mult), reads=["bi", "zi"], writes=["tq2"])
        P.op("dve", f_tt(Bb[0:G, 0], tq[0:G], tq2[0:G], ALU.subtract), reads=["tq", "tq2"], writes=["Bb0"])
        P.op("dve", f_tt(tq[0:G], biT, bc_n(zr[0:G]), ALU.mult), reads=["bi", "zr", "Bb0"], writes=["tq"])
        P.op("dve", f_tt(tq2[0:G], brT, bc_n(zi[0:G]), ALU.mult), reads=["br", "zi", "Bb0"], writes=["tq2"])
        P.op("dve", f_tt(Bb[0:G, 1], tq[0:G], tq2[0:G], ALU.add), reads=["tq", "tq2"], writes=["Bb1"])
        Ws = A.alloc((2, 2, 16, 64), BF16)
        for s in range(8):
            k = 7 - s
            sl = s % 2
            lr, li = bc_n(Lr[0:G, k, :]), bc_n(Li[0:G, k, :])
            P.op("dve", f_tt(tq[0:G], Bb[0:G, 0], lr, ALU.mult), reads=["Bb0", "Lr"], writes=["tq"])
            P.op("dve", f_tt(tq2[0:G], Bb[0:G, 1], li, ALU.mult), reads=["Bb1", "Li"], writes=["tq2"])
            P.op("dve", f_tt(Ws[0:G, sl, 0], tq[0:G], tq2[0:G], ALU.subtract), reads=["tq", "tq2"], writes=["Ws%d" % sl])
            P.op("dve", f_tt(tq[0:G], Bb[0:G, 1], lr, ALU.mult), reads=["Bb1", "Lr", "Ws%d" % sl], writes=["tq"])
            P.op("dve", f_tt(tq2[0:G], Bb[0:G, 0], li, ALU.mult), reads=["Bb0", "Li", "Ws%d" % sl], writes=["tq2"])
            P.op("dve", f_tt(Ws[0:G, sl, 1], tq[0:G], tq2[0:G], ALU.add), reads=["tq", "tq2", "Ws%d" % sl], writes=["Ws%d" % sl])
            P.op("sp", f_dma(self.Wd[:, s], Ws[0:G, sl]), reads=["Ws%d" % sl], writes=["Wd"], chan=self.chan("wd%d" % sl))
        ab = A.alloc((2, 64))
        P.op("dve", f_ts(ab[0:G, 0, :], ardt[0:G], 8.0, None, ALU.mult), reads=["ardt"], writes=["ab"])
        P.op("dve", f_ts(ab[0:G, 1, :], aidt[0:G], 8.0, None, ALU.mult), reads=["aidt", "ab"], writes=["ab"])
        P.op("sp", f_dma(self.abd.rearrange("a g n -> g a n"), ab[0:G]), reads=["ab"], writes=["abd"], chan=self.chan("abd"))
        rt = A.alloc((6, 64))
        for i, idx in enumerate((9, 10, 11)):
            P.op("dve", f_copy(rt[0:G, 2 * i, :], Lr[0:G, idx, :]), reads=["Lr"], writes=["rt"])
            P.op("dve", f_copy(rt[0:G, 2 * i + 1, :], Li[0:G, idx, :]), reads=["Li", "rt"], writes=["rt"])
        rt2 = A.alloc((6, 128))
        P.op("dve", f_copy(rt2[0:G, :, 0:64], rt[0:G]), reads=["rt"], writes=["rt2"])
        P.op("dve", f_copy(rt2[0:G, :, 64:128], rt[0:G]), reads=["rt", "rt2"], writes=["rt2"])
        for comp in range(6):
            P.op("pe", f_tr(ps[6][:, 0:G], rt2[0:G, comp, :], identF[0:G, 0:G]), reads=["rt2", "identF"], writes=["ps6"])
            v = ps[6][:, 0:G].rearrange("p (gt q e) -> p gt q e", q=4, e=2)
            P.op("dve", f_copy(self.RT[0:64, :, comp, :], v[0:64, :, :, 0]), reads=["ps6"], writes=["RT"])
            P.op("dve", f_copy(self.RT[64:128, :, comp, :], v[64:128, :, :, 1]), reads=["ps6", "RT"], writes=["RT"])
        yield
        for gl in range(8):
            for gt in range(8):
                src = self.Wd[gt * 8 + gl].rearrange("s c q n -> q (s c) n")
                dst = self.Wc[gl * 16:(gl + 1) * 16, gt].rearrange("p s c n -> p (s c) n")
                P.op("sp", f_dma(dst, src), reads=["Wd"], writes=["Wc"], chan=self.chan("wc"))
        LAn = A.alloc((9, 128)); LBn = A.alloc((9, 128))
        P.op("dve", f_copy(LAn[0:G, :, 0:64], Lr[0:G, 0:9, :]), reads=["Lr"], writes=["LAn"])
        P.op("dve", f_ts(LAn[0:G, :, 64:128], Lr[0:G, 0:9, :], -1.0, None, ALU.mult), reads=["Lr", "LAn"], writes=["LAn"])
        P.op("dve", f_copy(LBn[0:G, :, 0:64], Li[0:G, 0:9, :]), reads=["Li"], writes=["LBn"])
        P.op("dve", f_copy(LBn[0:G, :, 64:128], Li[0:G, 0:9, :]), reads=["Li", "LBn"], writes=["LBn"])
        LA = A.alloc((9, 64)); LB = A.alloc((9, 64))
        for k in range(9):
            for (src, dst, key, bank) in ((LAn, LA, "LA", 0), (LBn, LB, "LB", 7)):
                P.op("pe", f_tr(ps[bank][:, 0:G], src[0:G, k, :], identF[0:G, 0:G]), reads=[key + "n", "identF"], writes=["ps%d" % bank])
                P.op("act", f_act(dst[:, k, :], ps[bank][:, 0:G], ACTF.Identity), reads=["ps%d" % bank], writes=[key])
            yield
        Cab = A.alloc((8, 128)); Cba = A.alloc((8, 128))
        cre = self.c_re.rearrange("(gt gl) p n -> (gl p) gt n", gl=8)
        cim = self.c_im.rearrange("(gt gl) p n -> (gl p) gt n", gl=8)
        P.op("sp", f_dma(Cab[:, :, 0:64], cre), writes=["Cab"], chan=self.chan("ssmld6"))
        P.op("sp", f_dma(Cab[:, :, 64:128], cim), writes=["Cab"], chan=self.chan("ssmld7"))
        P.op("sp", f_dma(Cba[:, :, 0:64], cim), writes=["Cba"], chan=self.chan("ssmld8"))
        P.op("sp", f_dma(Cba[:, :, 64:128], cre), writes=["Cba"], chan=self.chan("ssmld9"))
        CTa = A.alloc((8, 128)); CTb = A.alloc((8, 128))
        for gt in range(8):
            for (src, dst, key, bank) in ((Cab, CTa, "CTa", 2), (Cba, CTb, "CTb", 3)):
                P.op("pe", f_tr(ps[bank][:, 0:128], src[:, gt, :], identF), reads=["Cab" if key == "CTa" else "Cba", "identF"], writes=["ps%d" % bank])
                P.op("act", f_act(dst[:, gt, :], ps[bank][:, 0:128], ACTF.Identity), reads=["ps%d" % bank], writes=[key])
            yield
        CTa4 = CTa.rearrange("m gt (gl p) -> m (gt gl) p", p=16)
        CTb4 = CTb.rearrange("m gt (gl p) -> m (gt gl) p", p=16)
        wt1 = A.alloc((64, 16)); wt2 = A.alloc((64, 16))
        for k in range(9):
            la = LA[:, k, :].unsqueeze(2).to_broadcast([128, 64, 16])
            lb = LB[:, k, :].unsqueeze(2).to_broadcast([128, 64, 16])
            P.op("dve", f_tt(wt1, CTa4, la, ALU.mult), reads=["CTa", "LA", "WoutK"], writes=["wt1"])
            P.op("dve", f_tt(wt2, CTb4, lb, ALU.mult), reads=["CTb", "LB", "WoutK"], writes=["wt2"])
            P.op("dve", f_tt(self.WoutK[:, :, k, :], wt1, wt2, ALU.subtract), reads=["wt1", "wt2"], writes=["WoutK"])
        BT = A.alloc((128,), BF16); BTp = A.alloc((8, 128), BF16)
        psb = ps[4].bitcast(BF16)
        for gt in range(8):
            P.op("pe", f_tr(psb[:, 0:128], self.Wc[:, gt, 7].rearrange("p c n -> p (c n)"), identB), reads=["Wc", "identB"], writes=["ps4"])
            P.op("act", f_act(BT, psb[:, 0:128], ACTF.Identity), reads=["ps4"], writes=["BT"])
            P.op("dve", f_tt(BTp, BT.unsqueeze(1).to_broadcast([128, 8, 128]), self.colmask, ALU.mult), reads=["BT", "colmask"], writes=["BTp"])
            for gl in range(8):
                g = gt * 8 + gl
                P.op("pe", f_mm(ps[5][:, 0:128], BTp[:, gl, :], self.WoutK[:, g, 0:8, :].rearrange("m k p -> m (k p)"), gl == 0, gl == 7),
                     reads=["BTp", "WoutK"], writes=["ps5"])
            P.op("act", f_act(self.Kc[:, gt].rearrange("p k q -> p (k q)"), ps[5][:, 0:128], ACTF.Identity), reads=["ps5"], writes=["Kc"])
            yield
        P.op("sp", f_dma(self.rows[0:8, :], self.ssm_d.rearrange("(gt gl) p -> gt (gl p)", gl=8)), writes=["rows"], chan=self.chan("rows"))
        P.op("pe", f_tr(ps[6][:, 0:8], self.rows[0:8, :], identF[0:8, 0:8]), reads=["rows", "identF"], writes=["ps6"])
        P.op("dve", f_copy(self.Dcol, ps[6][:, 0:8]), reads=["ps6"], writes=["Dcol"])
        self.dump("Wc", self.Wc, [128, 8, 8, 2, 64], ["Wc"])
        self.dump("Kc", self.Kc, [128, 8, 8, 16], ["Kc"])
        self.dump("WoutK", self.WoutK, [128, 64, 9, 16], ["WoutK"])
        P.barrier()
        A.reset(res_mark)

    def norm_bufs(self, with_bias):
        A = self.A
        self.nb_xn = [A.alloc((D,), BF16) for _ in range(2)]
        self.nb_tmp = A.alloc((D,)) if with_bias else None
        self.nb_ss = A.alloc((4,))
        self.nb_cnt = 0

    def norm_to_fm(self, src, src_key, dst_fn, dst_key, scale_bc, sck, bias_bc=None, bk=None):
        P, ps = self.P, self.ps
        i = self.nb_cnt % 2
        self.nb_cnt += 1
        ss = self.nb_ss[:, 2 * i:2 * i + 1]
        rs = self.nb_ss[:, 2 * i + 1:2 * i + 2]
        xn = self.nb_xn[i]
        xk = "nbxn%d" % i
        P.op("act", f_act(xn, src, ACTF.Square, accum_out=ss), reads=[src_key], writes=[xk, "nbss%d" % i])
        P.op("act", f_act(rs, ss, ACTF.Sqrt, bias=1e-6, scale=1.0 / D), reads=["nbss%d" % i], writes=["nbrs%d" % i])
        P.op("dve", f_recip(rs, rs), reads=["nbrs%d" % i], writes=["nbrs%d" % i])
        if bias_bc is None:
            P.op("dve", f_stt(xn, src, rs, scale_bc, ALU.mult, ALU.mult), reads=[src_key, "nbrs%d" % i, sck], writes=[xk])
        else:
            P.op("dve", f_stt(self.nb_tmp, src, rs, scale_bc, ALU.mult, ALU.mult), reads=[src_key, "nbrs%d" % i, sck], writes=["nbtmp"])
            P.op("dve", f_tt(xn, self.nb_tmp, bias_bc, ALU.add), reads=["nbtmp", bk], writes=[xk])
        for half in range(2):
            bank = 6 + half
            psb = ps[bank].bitcast(BF16)
            for q in range(8):
                kt = half * 8 + q
                P.op("pe", f_tr(psb[:, q * 128:(q + 1) * 128], xn[:, kt * 128:(kt + 1) * 128], self.identB),
                     reads=[xk, "identB"], writes=["ps%d" % bank])
            src3 = psb.rearrange("p (a b) -> p a b", b=128)
            if half == 0:
                P.op("act", f_act(dst_fn(half), src3, ACTF.Identity), reads=["ps%d" % bank], writes=[dst_key])
            else:
                P.op("dve", f_copy(dst_fn(half), src3), reads=["ps%d" % bank], writes=[dst_key])

    def phase_u(self):
        A, P, ps = self.A, self.P, self.ps
        self.GY, self.GY_bytes = A.alloc_top((8, 1152), BF16)
        self.uT, self.uT_bytes = A.alloc_top((8, 4096), BF16)
        self.after_u_mark = A.mark()
        Wu = A.alloc((16, 1024), BF16)
        P.op("pool", f_dma(Wu, self.w_in.rearrange("(kt p) m -> p kt m", p=128)[:, :, 1280:2304]), writes=["Wu"], chan=self.chan("Wu"))
        s1bc = A.alloc((D,), BF16)
        self.gbc_build(s1bc, 0, vec=self.sp[:, 0, :], key="s1bc")
        sh1b = A.alloc((16,), BF16)
        cu = A.alloc((8,))
        P.op("dve", f_copy(sh1b, self.MOD[:, 0:16]), reads=["MOD"], writes=["sh1b"])
        for m in range(8):
            for kt in range(KT):
                P.op("pe", f_mm(ps[0][:, m:m + 1], Wu[:, kt, m * 128:(m + 1) * 128], sh1b[:, kt:kt + 1], kt == 0, kt == KT - 1, True),
                     reads=["Wu", "sh1b"], writes=["ps0"])
        P.op("dve", f_copy(cu, ps[0][:, 0:8]), reads=["ps0"], writes=["cu"])
        self.norm_bufs(False)
        xt = [A.alloc((D,)) for _ in range(2)]
        hTu = [A.alloc((16, 384), BF16), A.alloc((16, 256), BF16)]
        groups = []
        w_ = 0
        while w_ < NW:
            nt_ = min(3 if len(groups) % 2 == 0 else 2, NW - w_)
            groups.append((w_, nt_))
            w_ += nt_
        for gi, (w0, nt) in enumerate(groups):
            hs = gi % 2
            N = nt * 128
            for ti in range(nt):
                w = w0 + ti
                xs = w % 2
                P.op("sp", f_dma(xt[xs], self.xw[w * 128:(w + 1) * 128, :]), writes=["xt%d" % xs], chan=self.chan("xt%d" % xs))
                self.norm_to_fm(xt[xs], "xt%d" % xs, lambda half, hs=hs, ti=ti: hTu[hs][:, half * 8:(half + 1) * 8, ti * 128:(ti + 1) * 128],
                                "hTu%d" % hs, s1bc, "s1bc")
            mask = self.cst[:, w0:w0 + nt].unsqueeze(2).to_broadcast([128, nt, 128])
            for m in range(8):
                bank = 1 + m % 4
                for kt in range(KT):
                    P.op("pe", f_mm(ps[bank][:, 0:N], Wu[:, kt, m * 128:(m + 1) * 128], hTu[hs][:, kt, 0:N], kt == 0, kt == KT - 1),
                         reads=["Wu", "hTu%d" % hs], writes=["ps%d" % bank])
                dst = self.uT[:, m, w0 * 128:w0 * 128 + N].rearrange("p (a b) -> p a b", b=128)
                P.op("dve", f_stt(dst, ps[bank][:, 0:N].rearrange("p (a b) -> p a b", b=128), cu[:, m:m + 1], mask, ALU.add, ALU.mult),
                     reads=["ps%d" % bank, "cst", "cu"], writes=["uT"])
        self.dump("uT", self.uT, [128, 8, 4096], ["uT"])
        P.barrier()
        A.reset(self.after_u_mark)

    def cmul_cols(self, outr, outi, ko, ar, ai, br, bi, t1, t2, keys_r, accumulate=False):
        P = self.P
        e = "dve"
        if not accumulate:
            P.op(e, f_tt(outr, ar, br, ALU.mult), reads=keys_r, writes=[ko])
            P.op(e, f_tt(t1, ai, bi, ALU.mult), reads=keys_r, writes=["cc_t1"])
            P.op(e, f_tt(outr, outr, t1, ALU.subtract), reads=[ko, "cc_t1"], writes=[ko])
            P.op(e, f_tt(outi, ar, bi, ALU.mult), reads=keys_r + [ko], writes=[ko])
            P.op(e, f_tt(t2, ai, br, ALU.mult), reads=keys_r, writes=["cc_t2"])
            P.op(e, f_tt(outi, outi, t2, ALU.add), reads=[ko, "cc_t2"], writes=[ko])
        else:
            P.op(e, f_tt(t1, ar, br, ALU.mult), reads=keys_r, writes=["cc_t1"])
            P.op(e, f_tt(outr, outr, t1, ALU.add), reads=[ko, "cc_t1"], writes=[ko])
            P.op(e, f_tt(t1, ai, bi, ALU.mult), reads=keys_r + [ko], writes=["cc_t1"])
            P.op(e, f_tt(outr, outr, t1, ALU.subtract), reads=[ko, "cc_t1"], writes=[ko])
            P.op(e, f_tt(t2, ar, bi, ALU.mult), reads=keys_r, writes=["cc_t2"])
            P.op(e, f_tt(outi, outi, t2, ALU.add), reads=[ko, "cc_t2"], writes=[ko])
            P.op(e, f_tt(t2, ai, br, ALU.mult), reads=keys_r + [ko], writes=["cc_t2"])
            P.op(e, f_tt(outi, outi, t2, ALU.add), reads=[ko, "cc_t2"], writes=[ko])

    def ssm_main(self):
        A, P, ps = self.A, self.P, self.ps
        identF, onesF = self.identF, self.onesF
        uT = self.uT
        self.after_gy_mark = A.mark()
        Wpads = [A.alloc((8, 2, 512), BF16) for _ in range(2)]
        Kpad = A.alloc((2, 1024), BF16)
        Etab = A.alloc((4, 512))
        T = [A.alloc((512,)) for _ in range(4)]
        mag, phi, sn, cs = T
        Xr = A.alloc((512,)); Xi = A.alloc((512,))
        Hc = A.alloc((2, 4)); Nc = A.alloc((2, 4)); Cc = A.alloc((2, 4)); ct1 = A.alloc((4,)); ct2 = A.alloc((4,))
        dg = A.alloc((8, 128))
        Pbuf = A.alloc((8, 2, 64))
        abb = Pbuf.rearrange("p g c n -> p (g c n)").rearrange("p (a b) -> p a b", b=512)
        PT = A.alloc((8, 128), BF16)
        Ysb = A.alloc((8, 128))
        ytmp = A.alloc((128,))
        sct = [(Xr, "Xr"), (Xi, "Xi"), (Ysb.rearrange("p a b -> p (a b)")[:, 0:512], "Ysb")]
        for gt in range(8):
            def build_wpad(g_):
                wp = Wpads[g_ % 2]
                for s_ in range(8):
                    o = wp[:, s_].rearrange("p c (g n) -> p c g n", n=64)
                    i0 = self.Wc[:, g_, s_].unsqueeze(2).to_broadcast([128, 2, 8, 64])
                    i1 = self.gmask.unsqueeze(1).unsqueeze(3).to_broadcast([128, 2, 8, 64])
                    P.op("pool", f_tt(o, i0, i1, ALU.mult), reads=["Wc", "gmask"], writes=["Wpad%d" % (g_ % 2)])

            if gt == 0:
                build_wpad(0)
            if gt + 1 < 8:
                build_wpad(gt + 1)
            Wpad = Wpads[gt % 2]
            wpk = "Wpad%d" % (gt % 2)
            for a in range(2):
                src = self.abd[a, gt * 8:(gt + 1) * 8, :].rearrange("g n -> (g n)").partition_broadcast(128)
                P.op("sp", f_dma(abb[:, a, :], src), reads=["abd"], writes=["Pbuf"], chan=self.chan("abb"))
            for ti, col in ((0, 42), (1, 43)):
                scol = self.cst[:, col:col + 1]
                P.op("act", f_act(mag, abb[:, 0, :], ACTF.Exp, scale=scol), reads=["Pbuf", "cst"], writes=["T0"])
                P.op("dve", f_ts(phi, abb[:, 1, :], scol, None, ALU.mult), reads=["Pbuf", "cst"], writes=["T1"])
                self.wrap_sincos("dve", phi, sct, "T1", [(sn, "sin", "T2"), (cs, "cos", "T3")])
                P.op("dve", f_tt(Etab[:, 2 * ti, :], mag, cs, ALU.mult), reads=["T0", "T3"], writes=["Etab"])
                P.op("dve", f_tt(Etab[:, 2 * ti + 1, :], mag, sn, ALU.mult), reads=["T0", "T2"], writes=["Etab"])
            P.op("dve", f_memset(Hc, 0.0), writes=["Hc"])
            Emr, Emi, Epr, Epi = Etab[:, 0, :], Etab[:, 1, :], Etab[:, 2, :], Etab[:, 3, :]
            RTg = self.RT[:, gt]

            def lhs(JT, s):
                return uT[:, gt, JT * 1024 + s:JT * 1024 + 1024:8]

            def stageA(JT):
                b0 = 0 if JT % 2 == 0 else 2
                for s in range(8):
                    P.op("pe", f_mm(ps[b0], lhs(JT, s), Wpad[:, s, 0, :], s == 0, s == 7), reads=["uT", wpk], writes=["ps%d" % b0])
                    P.op("pe", f_mm(ps[b0 + 1], lhs(JT, s), Wpad[:, s, 1, :], s == 0, s == 7), reads=["uT", wpk], writes=["ps%d" % (b0 + 1)])

            def stageB(JT):
                b0 = 0 if JT % 2 == 0 else 2
                k0, k1 = "ps%d" % b0, "ps%d" % (b0 + 1)
                P.op("dve", f_tt(T[0], ps[b0], Emr, ALU.mult), reads=[k0, "Etab"], writes=["T0"])
                P.op("dve", f_tt(T[1], ps[b0 + 1], Emi, ALU.mult), reads=[k1, "Etab"], writes=["T1"])
                P.op("dve", f_tt(Xr, T[0], T[1], ALU.subtract), reads=["T0", "T1"], writes=["Xr"])
                P.op("dve", f_tt(T[2], ps[b0], Emi, ALU.mult), reads=[k0, "Etab"], writes=["T2"])
                P.op("dve", f_tt(T[3], ps[b0 + 1], Emr, ALU.mult), reads=[k1, "Etab"], writes=["T3"])
                P.op("dve", f_tt(Xi, T[2], T[3], ALU.add), reads=["T2", "T3"], writes=["Xi"])
                cc0 = (JT % 2) * 8
                for c, (X, xk) in enumerate(((Xr, "Xr"), (Xi, "Xi"))):
                    for q in range(4):
                        col = cc0 + c * 4 + q
                        P.op("pe", f_mm(ps[7][:, col:col + 1], X[:, q * 128:(q + 1) * 128], onesF[:, 0:1], True, True, True),
                             reads=["onesF", xk], writes=["ps7"])

            def stageC(JT):
                full = JT >= 2
                tok0 = JT * 1024
                cc0 = (JT % 2) * 8
                if full:
                    self.cmul_cols(Cc[:, 0, :], Cc[:, 1, :], "Cc", RTg[:, 4, :], RTg[:, 5, :], Hc[:, 0, :], Hc[:, 1, :], ct1, ct2, ["RT", "Hc"])
                    for c in range(2):
                        for q in range(4):
                            P.op("dve", f_ts(dg[:, c * 4 + q, :], identF, Cc[:, c, q:q + 1], None, ALU.mult), reads=["identF", "Cc"], writes=["dg"])
                    for c, (X, xk, bank) in enumerate(((Xr, "Xr", 4), (Xi, "Xi", 5))):
                        P.op("pe", f_mm(ps[bank], self.Lstrict, X, True, False, True), reads=["Ls", xk], writes=["ps%d" % bank])
                        for q in range(4):
                            P.op("pe", f_mm(ps[bank][:, q * 128:(q + 1) * 128], onesF, dg[:, c * 4 + q, :], False, True, True),
                                 reads=["onesF", "dg"], writes=["ps%d" % bank])
                self.cmul_cols(Nc[:, 0, :], Nc[:, 1, :], "Nc", ps[7][:, cc0:cc0 + 4], ps[7][:, cc0 + 4:cc0 + 8], RTg[:, 0, :], RTg[:, 1, :], ct1, ct2, ["ps7", "RT"])
                self.cmul_cols(Nc[:, 0, :], Nc[:, 1, :], "Nc", Hc[:, 0, :], Hc[:, 1, :], RTg[:, 2, :], RTg[:, 3, :], ct1, ct2, ["Hc", "RT"], accumulate=True)
                P.op("dve", f_copy(Hc, Nc), reads=["Nc"], writes=["Hc"])
                if not full:
                    return
                Pr = Pbuf[:, :, 0, :]
                Pi = Pbuf[:, :, 1, :]
                v3 = lambda ap: ap.rearrange("p (g n) -> p g n", n=64)
                P.op("dve", f_tt(T[0], ps[4], Epr, ALU.mult), reads=["ps4", "Etab"], writes=["T0"])
                P.op("dve", f_tt(T[1], ps[5], Epi, ALU.mult), reads=["ps5", "Etab"], writes=["T1"])
                P.op("dve", f_tt(Pr, v3(T[0]), v3(T[1]), ALU.subtract), reads=["T0", "T1"], writes=["Pbuf"])
                P.op("dve", f_tt(T[2], ps[4], Epi, ALU.mult), reads=["ps4", "Etab"], writes=["T2"])
                P.op("dve", f_tt(T[3], ps[5], Epr, ALU.mult), reads=["ps5", "Etab"], writes=["T3"])
                P.op("dve", f_tt(Pi, v3(T[2]), v3(T[3]), ALU.add), reads=["T2", "T3", "Pbuf"], writes=["Pbuf"])
                for gl in range(8):
                    bank = 6 + (gl // 4)
                    P.op("pe", f_tr(ps[bank][:, (gl % 4) * 128:(gl % 4 + 1) * 128], Pbuf[:, gl].rearrange("p c n -> p (c n)"), identF),
                         reads=["Pbuf", "identF"], writes=["ps%d" % bank])
                for gl in range(8):
                    bank = 6 + (gl // 4)
                    P.op("act", f_act(PT[:, gl, :], ps[bank][:, (gl % 4) * 128:(gl % 4 + 1) * 128], ACTF.Identity),
                         reads=["ps%d" % bank], writes=["PT"])
                for s in range(8):
                    ks = s % 2
                    P.op("pool", f_memset(Kpad[:, ks, :], 0.0), writes=["Kpad%d" % ks])
                    o = Kpad[:, ks, :].rearrange("p (g t q) -> p g t q", t=8, q=16)[:, :, s:8, :]
                    i0 = self.Kc[:, gt, 0:8 - s, :].unsqueeze(1).to_broadcast([128, 8, 8 - s, 16])
                    i1 = self.gmask.unsqueeze(2).unsqueeze(3).to_broadcast([128, 8, 8 - s, 16])
                    P.op("pool", f_tt(o, i0, i1, ALU.mult), reads=["Kc", "gmask", "Kpad%d" % ks], writes=["Kpad%d" % ks])
                    P.op("pe", f_mm(ps[4], lhs(JT, s), Kpad[:, ks, 0:512], s == 0, False, True), reads=["uT", "Kpad%d" % ks], writes=["ps4"])
                    P.op("pe", f_mm(ps[5], lhs(JT, s), Kpad[:, ks, 512:1024], s == 0, False, True), reads=["uT", "Kpad%d" % ks], writes=["ps5"])
                for gl in range(8):
                    bank = 4 + gl // 4
                    g = gt * 8 + gl
                    P.op("pe", f_mm(ps[bank][:, (gl % 4) * 128:(gl % 4 + 1) * 128], PT[:, gl, :],
                                    self.WoutK[:, g, 1:9, :].rearrange("m k p -> m (k p)"), False, True, True),
                         reads=["PT", "WoutK"], writes=["ps%d" % bank])
                for b in range(2):
                    o = Ysb[:, :, b * 64:(b + 1) * 64].rearrange("p t (g q) -> p t g q", q=16)
                    i = ps[4 + b].rearrange("p (g t q) -> p t g q", t=8, q=16)
                    P.op("act" if b == 0 else "dve", f_copy(o, i) if b else f_act(o, i, ACTF.Identity), reads=["ps%d" % (4 + b)], writes=["Ysb"])
                c0 = 112 if JT == 2 else 0
                for tp in range(8):
                    bank = 6 + tp % 2
                    P.op("pe", f_tr(ps[bank][:, 0:128], Ysb[:, tp, :], identF), reads=["Ysb", "identF"], writes=["ps%d" % bank])
                    uin = uT[:, gt, tok0 + 8 * c0 + tp:tok0 + 1024:8]
                    P.op("dve", f_stt(ytmp[:, c0:128], uin, self.Dcol[:, gt:gt + 1], ps[bank][:, c0:128], ALU.mult, ALU.add),
                         reads=["uT", "Dcol", "ps%d" % bank], writes=["ytmp"])
                    if JT == 2:
                        dst = self.GY[:, gt, tp:128:8]
                    else:
                        dst = self.GY[:, gt, 128 + tp:1152:8]
                    P.op("act", f_act(dst, ytmp[:, c0:128], ACTF.Gelu), reads=["ytmp"], writes=["GY"])

            stageA(0); stageA(1); stageB(0); stageA(2); stageB(1); stageC(0)
            stageA(3); stageB(2); stageC(1); stageC(2); stageB(3); stageC(3)
        self.dump("GY", self.GY, [128, 8, 1152], ["GY"])
        P.barrier()
        A.reset(self.after_gy_mark)
        A.free_top(self.uT_bytes)

    def gbc_build(self, dst, c0, vec=None, key="gbc"):
        P, ps = self.P, self.ps
        src = self.MOD[:, c0:c0 + 16] if vec is None else vec
        rk = ["identF", "MOD", "sp"]
        for kt in range(KT):
            b = kt % 2
            P.op("dve", f_ts(self.diag[:, b, :], self.identF, src[:, kt:kt + 1], None, ALU.mult), reads=rk, writes=["diag%d" % b])
            bank = 4 + kt // 4
            P.op("pe", f_mm(ps[bank][:, (kt % 4) * 128:(kt % 4 + 1) * 128], self.onesF, self.diag[:, b, :]),
                 reads=["onesF", "diag%d" % b], writes=["ps%d" % bank])
        for q in range(4):
            P.op("act", f_act(dst[:, q * 512:(q + 1) * 512], ps[4 + q], ACTF.Identity), reads=["ps%d" % (4 + q)], writes=[key])

    def mixer(self):
        A, P, ps = self.A, self.P, self.ps
        identB = self.identB
        P.barrier()
        A.reset(self.res0_mark)
        MIXbuf = A.alloc((16, 1280), BF16)
        MIX = MIXbuf[:, :, 0:1152]
        mix_end = A.mark()
        hT = A.alloc((16, 1280), BF16)
        qT = A.alloc((8, 1152), BF16)
        kd = A.alloc((2, 1280), BF16)
        vpad = A.alloc((4, 10, 128), BF16)
        ph_mark = A.mark()
        s1bc = A.alloc((D,), BF16); sh1bc = A.alloc((D,), BF16)
        self.gbc_build(s1bc, 0, vec=self.sp[:, 0, :], key="s1bc")
        self.gbc_build(sh1bc, 0, key="sh1bc")
        self.norm_bufs(True)
        xt = [A.alloc((D,)) for _ in range(2)]
        for w in range(22, 32):
            xs = w % 2
            P.op("sp", f_dma(xt[xs], self.xw[w * 128:(w + 1) * 128, :]), writes=["xt%d" % xs], chan=self.chan("xt%d" % xs))
            self.norm_to_fm(xt[xs], "xt%d" % xs, lambda half, w=w: hT[:, half * 8:(half + 1) * 8, (w - 22) * 128:(w - 21) * 128], "hT",
                            s1bc, "s1bc", sh1bc, "sh1bc")
        P.barrier()
        A.reset(ph_mark)
        NS = 4
        wsl = [A.alloc((16, 128), BF16) for _ in range(NS)]
        vT = A.alloc((1280,), BF16)
        vtm = A.alloc((128,), BF16)
        win = self.w_in.rearrange("(kt p) m -> p kt m", p=128)
        jobs = [("q", i, i * 128) for i in range(8)] + [("k", 0, 1024), ("k", 1, 1088), ("v", 0, 1152)]

        def wload(ji):
            kind, idx, c0 = jobs[ji]
            s = ji % NS
            if kind == "k":
                P.op("pool", f_dma(wsl[s][:, :, 0:64], win[:, :, c0:c0 + 64]), writes=["wsl%d" % s], chan=self.chan("wsl%d" % s))
                P.op("pool", f_dma(wsl[s][:, :, 64:128], win[:, :, c0:c0 + 64]), writes=["wsl%d" % s], chan=self.chan("wsl%d" % s))
            else:
                P.op("pool", f_dma(wsl[s], win[:, :, c0:c0 + 128]), writes=["wsl%d" % s], chan=self.chan("wsl%d" % s))

        for ji in range(min(NS, len(jobs))):
            wload(ji)
        ev = 0
        for ji, (kind, idx, c0) in enumerate(jobs):
            s = ji % NS
            chunks = [(128, 640), (640, 1152), (1152, 1280)] if kind == "q" else [(0, 512), (512, 1024), (1024, 1280)]
            for ci, (a, b) in enumerate(chunks):
                bank = ev % 4
                for kt in range(KT):
                    P.op("pe", f_mm(ps[bank][:, 0:b - a], wsl[s][:, kt, :], hT[:, kt, a:b], kt == 0, kt == KT - 1),
                         reads=["wsl%d" % s, "hT"], writes=["ps%d" % bank])
                if kind == "q":
                    dst, dk = qT[:, idx, a - 128:b - 128], "q%d" % idx
                elif kind == "k":
                    dst, dk = kd[:, idx, a:b], "kd"
                else:
                    dst, dk = vT[:, a:b], "vT"
                if ev % 2 == 0:
                    P.op("act", f_act(dst, ps[bank][:, 0:b - a], ACTF.Identity), reads=["ps%d" % bank], writes=[dk])
                else:
                    P.op("dve", f_copy(dst, ps[bank][:, 0:b - a]), reads=["ps%d" % bank], writes=[dk])
                ev += 1
            if ji + NS < len(jobs):
                wload(ji + NS)
        P.op("dve", f_memset(vpad, 0.0), writes=["vpad"])
        psb = ps[6].bitcast(BF16)
        for kc in range(10):
            P.op("pe", f_tr(psb[:, 0:128], vT[:, kc * 128:(kc + 1) * 128], identB), reads=["vT", "identB"], writes=["ps6"])
            P.op("act", f_act(vtm, psb[:, 0:128], ACTF.Identity), reads=["ps6"], writes=["vtm"])
            for h in range(2):
                for e in range(2):
                    P.op("dve", f_copy(vpad[:, h * 2 + e, kc, 64 * e:64 * e + 64], vtm[:, 64 * h:64 * h + 64]), reads=["vtm", "vpad"], writes=["vpad"])
        self.dump("qT", qT, [128, 8, 1152], ["q%d" % i for i in range(8)])
        self.dump("kd", kd, [128, 2, 1280], ["kd"])
        srow = A.alloc((16,))
        es2 = A.alloc((8,))
        P.op("sp", f_dma(srow[0:1], self.sinks.rearrange("(o a) -> o a", o=1)), writes=["srow"], chan=self.chan("srow"))
        P.op("act", f_act(srow[0:1], srow[0:1], ACTF.Exp), reads=["srow"], writes=["srow"])
        P.op("pe", f_mm(ps[7][:, 0:16], self.onesF[0:1, :], srow[0:1]), reads=["onesF", "srow"], writes=["ps7"])
        P.op("dve", f_copy(es2[0:64], ps[7][0:64, 0:16:2]), reads=["ps7"], writes=["es2"])
        P.op("dve", f_copy(es2[64:128], ps[7][64:128, 1:16:2]), reads=["ps7", "es2"], writes=["es2"])
        Pb = [A.alloc((2, 10, 256), BF16) for _ in range(2)]
        rr = A.alloc((128,))
        sc = 0
        for i in range(8):
            h = i // 4
            pb = Pb[i % 2]
            pk = "Pb%d" % (i % 2)
            for kc in range(10):
                kb = kc - 1
                if kb == -1:
                    qa, qb_, m0, m1, o0 = 0, 128, 128, 256, 128
                elif kb == 8:
                    qa, qb_, m0, m1, o0 = 1024, 1152, 0, 128, 0
                else:
                    qa, qb_, m0, m1, o0 = kb * 128, kb * 128 + 256, 0, 256, 0
                N = qb_ - qa
                for e in range(2):
                    bank = sc % 2
                    sc += 1
                    P.op("pe", f_mm(ps[bank][:, 0:N], kd[64 * e:64 * e + 64, h, kc * 128:(kc + 1) * 128], qT[64 * e:64 * e + 64, i, qa:qb_], True, True),
                         reads=["kd", "q%d" % i], writes=["ps%d" % bank])
                    pk2 = pk + "_%d_%d" % (e, kc % 2)
                    P.op("act", f_act(pb[:, e, kc, o0:o0 + N], ps[bank][:, 0:N], ACTF.Exp, bias=self.cst[:, 32 + kc:33 + kc], scale=0.125),
                         reads=["ps%d" % bank, "cst"], writes=[pk2])
                    P.op("dve", f_tt(pb[:, e, kc, o0:o0 + N], pb[:, e, kc, o0:o0 + N], self.mask01[:, m0:m1], ALU.mult),
                         reads=[pk2, "mask01"], writes=[pk2])
            for n in range(9):
                bo, bd = 2 + n % 2, 4 + n % 2
                terms = []
                for e in range(2):
                    terms.append((e, n, slice(128, 256)))
                    terms.append((e, n + 1, slice(0, 128)))
                for ti, (e, kc, sl) in enumerate(terms):
                    P.op("pe", f_mm(ps[bo][:, 0:128], vpad[:, h * 2 + e, kc, :], pb[:, e, kc, sl], ti == 0, ti == 3), reads=["vpad", pk + "_%d_%d" % (e, kc % 2)], writes=["ps%d" % bo])
                for ti, (e, kc, sl) in enumerate(terms):
                    P.op("pe", f_mm(ps[bd][:, 0:128], self.onespad[:, e, :], pb[:, e, kc, sl], ti == 0, ti == 3), reads=["onespad", pk + "_%d_%d" % (e, kc % 2)], writes=["ps%d" % bd])
                P.op("dve", f_ts(rr, ps[bd][:, 0:128], es2[:, i:i + 1], None, ALU.add), reads=["ps%d" % bd, "es2"], writes=["rr"])
                P.op("dve", f_recip(rr, rr), reads=["rr"], writes=["rr"])
                P.op("dve", f_tt(qT[:, i, n * 128:(n + 1) * 128], ps[bo][:, 0:128], rr, ALU.mult), reads=["ps%d" % bo, "rr"], writes=["q%d" % i])
        self.dump("attn", qT, [128, 8, 1152], ["q%d" % i for i in range(8)])
        P.barrier()
        A.reset(ph_mark)
        GYb = self.GY
        wg = [A.alloc((56, 128), BF16) for _ in range(2)]
        tA = A.alloc((512,), BF16); tS = A.alloc((512,), BF16); tB = A.alloc((512,)); tT = A.alloc((512,))
        wap = self.w_ap.rearrange("(kt p) m -> p kt m", p=128)
        wgl = self.w_glu.rearrange("(kt p) m -> p kt m", p=128)

        def gload(mt):
            s = mt % 2
            k, c = "wg%d" % s, self.chan("wg%d" % s)
            P.op("pool", f_dma(wg[s][:, 0:16, :], win[:, :, 2304 + mt * 128:2304 + (mt + 1) * 128]), writes=[k], chan=c)
            P.op("pool", f_dma(wg[s][:, 16:32, :], win[:, :, 4352 + mt * 128:4352 + (mt + 1) * 128]), writes=[k], chan=c)
            P.op("pool", f_dma(wg[s][:, 32:40, :], wap[:, :, mt * 128:(mt + 1) * 128]), writes=[k], chan=c)
            P.op("pool", f_dma(wg[s][:, 40:48, :], wgl[:, :, mt * 128:(mt + 1) * 128]), writes=[k], chan=c)
            P.op("pool", f_dma(wg[s][:, 48:56, :], wgl[:, :, D + mt * 128:D + (mt + 1) * 128]), writes=[k], chan=c)

        gload(0)
        gload(1)
        for mt in range(16):
            s = mt % 2
            k = "wg%d" % s
            for (a, b) in [(0, 512), (512, 1024), (1024, 1152)]:
                n = b - a
                for kt in range(16):
                    P.op("pe", f_mm(ps[0][:, 0:n], wg[s][:, kt, :], hT[:, kt, 128 + a:128 + b], kt == 0, kt == 15), reads=[k, "hT"], writes=["ps0"])
                P.op("act", f_act(tA[:, 0:n], ps[0][:, 0:n], ACTF.Sigmoid), reads=["ps0"], writes=["tA"])
                for kt in range(8):
                    P.op("pe", f_mm(ps[1][:, 0:n], wg[s][:, 32 + kt, :], qT[:, kt, a:b], kt == 0, kt == 7), reads=[k] + ["q%d" % kt], writes=["ps1"])
                P.op("dve", f_tt(MIX[:, mt, a:b], ps[1][:, 0:n], tA[:, 0:n], ALU.mult), reads=["ps1", "tA"], writes=["MIX"])
                for kt in range(16):
                    P.op("pe", f_mm(ps[2][:, 0:n], wg[s][:, 16 + kt, :], hT[:, kt, 128 + a:128 + b], kt == 0, kt == 15), reads=[k, "hT"], writes=["ps2"])
                P.op("act", f_act(tS[:, 0:n], ps[2][:, 0:n], ACTF.Sigmoid), reads=["ps2"], writes=["tS"])
                for kt in range(8):
                    P.op("pe", f_mm(ps[3][:, 0:n], wg[s][:, 40 + kt, :], GYb[:, kt, a:b], kt == 0, kt == 7), reads=[k, "GY"], writes=["ps3"])
                for kt in range(8):
                    P.op("pe", f_mm(ps[4][:, 0:n], wg[s][:, 48 + kt, :], GYb[:, kt, a:b], kt == 0, kt == 7), reads=[k, "GY"], writes=["ps4"])
                P.op("act", f_act(tB[:, 0:n], ps[4][:, 0:n], ACTF.Sigmoid), reads=["ps4"], writes=["tB"])
                P.op("dve", f_tt(tT[:, 0:n], ps[3][:, 0:n], tB[:, 0:n], ALU.mult), reads=["ps3", "tB"], writes=["tT"])
                P.op("dve", f_tt(tT[:, 0:n], tT[:, 0:n], tS[:, 0:n], ALU.mult), reads=["tT", "tS"], writes=["tT"])
                P.op("dve", f_tt(MIX[:, mt, a:b], MIX[:, mt, a:b], tT[:, 0:n], ALU.add), reads=["MIX", "tT"], writes=["MIX"])
            if mt + 2 < 16:
                gload(mt + 2)
        self.dump("MIX", MIX, [128, 16, 1152], ["MIX"])
        P.barrier()
        A.reset(mix_end)
        A.free_top(self.GY_bytes)
        self.x1, self.x1_bytes = A.alloc_top((9, D))
        x1 = self.x1
        g1bc = A.alloc((D,))
        self.gbc_build(g1bc, 32)
        Wb = [A.alloc((16, 512), BF16) for _ in range(2)]
        xc = [A.alloc((512,)) for _ in range(2)]
        wo = self.w_out.rearrange("(kt p) m -> p kt m", p=128)
        P.op("pool", f_dma(Wb[0], wo[:, :, 0:512]), writes=["Wb0"], chan=self.chan("Wb0"))
        cnt = 0
        for cb in range(4):
            s = cb % 2
            if cb + 1 < 4:
                P.op("pool", f_dma(Wb[1 - s], wo[:, :, (cb + 1) * 512:(cb + 2) * 512]), writes=["Wb%d" % (1 - s)], chan=self.chan("Wb%d" % (1 - s)))
            for n in range(9):
                bank = cnt % 4
                xs = cnt % 2
                cnt += 1
                P.op("sp", f_dma(xc[xs], self.xw[(23 + n) * 128:(24 + n) * 128, cb * 512:(cb + 1) * 512]), writes=["xc%d" % xs], chan=self.chan("xc%d" % xs))
                for kt in range(16):
                    P.op("pe", f_mm(ps[bank], MIX[:, kt, n * 128:(n + 1) * 128], Wb[s][:, kt, :], kt == 0, kt == 15), reads=["MIX", "Wb%d" % s], writes=["ps%d" % bank])
                dst = x1[:, n, cb * 512:(cb + 1) * 512]
                P.op("dve", f_tt(dst, ps[bank], g1bc[:, cb * 512:(cb + 1) * 512], ALU.mult), reads=["ps%d" % bank, "gbc"], writes=["x1_%d" % n])
                P.op("dve", f_tt(dst, dst, xc[xs], ALU.add), reads=["x1_%d" % n, "xc%d" % xs], writes=["x1_%d" % n])
        self.dump("x1", x1, [128, 9, D], ["x1_%d" % n for n in range(9)])
        P.barrier()
        A.reset(mix_end)
        self.h2T = MIXbuf[:, :, 0:1152]
        s2bc = A.alloc((D,), BF16); sh2bc = A.alloc((D,), BF16)
        self.gbc_build(s2bc, 0, vec=self.sp[:, 1, :], key="s2bc")
        self.gbc_build(sh2bc, 48, key="sh2bc")
        self.norm_bufs(True)
        for n in range(9):
            self.norm_to_fm(x1[:, n, :], "x1_%d" % n, lambda half, n=n: self.h2T[:, half * 8:(half + 1) * 8, n * 128:(n + 1) * 128], "h2T",
                            s2bc, "s2bc", sh2bc, "sh2bc")
        P.barrier()
        A.reset(mix_end)

    def ffn(self):
        A, P, ps = self.A, self.P, self.ps
        identF = self.identF
        x1, h2T = self.x1, self.h2T
        cw = A.alloc((132,)); cbias = A.alloc((44,))
        cwd = self.conv_w.rearrange("k (m p) -> (k m) p", p=128)
        rows = self.rows
        for (r0, r1) in ((0, 128), (128, 132)):
            nr = r1 - r0
            P.op("sp", f_dma(rows[0:nr, :], cwd[r0:r1, :]), writes=["rows"], chan=self.chan("rows"))
            P.op("pe", f_tr(ps[0][:, 0:nr], rows[0:nr, :], identF[0:nr, 0:nr]), reads=["rows", "identF"], writes=["ps0"])
            P.op("dve", f_copy(cw[:, r0:r1], ps[0][:, 0:nr]), reads=["ps0"], writes=["cw"])
        P.op("sp", f_dma(rows[0:44, :], self.conv_b.rearrange("(m p) -> m p", p=128)), writes=["rows"], chan=self.chan("rows"))
        P.op("pe", f_tr(ps[0][:, 0:44], rows[0:44, :], identF[0:44, 0:44]), reads=["rows", "identF"], writes=["ps0"])
        P.op("dve", f_copy(cbias, ps[0][:, 0:44]), reads=["ps0"], writes=["cbias"])
        g2bc = A.alloc((D,))
        self.gbc_build(g2bc, 80)
        ACTB = A.alloc((11, 1024), BF16)
        GP = [A.alloc((1032,))] * 2
        cA = [A.alloc((1024,))] * 2
        sB = [A.alloc((1024,), BF16)] * 2
        wu = [A.alloc((2, 16, 128), BF16) for _ in range(2)]
        wd = [A.alloc((11, 512), BF16) for _ in range(2)]
        tmp = [A.alloc((512,)) for _ in range(2)]
        wup = self.w_up.rearrange("(kt p) m -> p kt m", p=128)
        wdn = self.w_down.rearrange("(kt p) m -> p kt m", p=128)

        def uload(mt):
            s = mt % 2
            P.op("pool", f_dma(wu[s][:, 0], wup[:, :, mt * 128:(mt + 1) * 128]), writes=["wu%d" % s], chan=self.chan("wu%d" % s))
            P.op("pool", f_dma(wu[s][:, 1], wup[:, :, 5632 + mt * 128:5632 + (mt + 1) * 128]), writes=["wu%d" % s], chan=self.chan("wu%d" % s))

        def dload(ch, cb, s):
            P.op("pool", f_dma(wd[s], wdn[:, ch * 11:(ch + 1) * 11, cb * 512:(cb + 1) * 512]), writes=["wd%d" % s], chan=self.chan("wd%d" % s))

        uload(0)
        uload(1)
        dcnt = 0
        for ch in range(4):
            for pr in range(11):
                mt = ch * 11 + pr
                s = mt % 2
                k = "wu%d" % s
                gp, ca, sb = GP[s], cA[s], sB[s]
                for kt in range(16):
                    P.op("pe", f_mm(ps[0], wu[s][:, 0, kt, :], h2T[:, kt, 128:640], kt == 0, kt == 15), reads=[k, "h2T"], writes=["ps0"])
                for kt in range(16):
                    P.op("pe", f_mm(ps[1], wu[s][:, 0, kt, :], h2T[:, kt, 640:1152], kt == 0, kt == 15), reads=[k, "h2T"], writes=["ps1"])
                for kt in range(16):
                    P.op("pe", f_mm(ps[2][:, 0:8], wu[s][:, 0, kt, :], h2T[:, kt, 120:128], kt == 0, kt == 15), reads=[k, "h2T"], writes=["ps2"])
                for kt in range(16):
                    P.op("pe", f_mm(ps[3], wu[s][:, 1, kt, :], h2T[:, kt, 128:640], kt == 0, kt == 15), reads=[k, "h2T"], writes=["ps3"])
                for kt in range(16):
                    P.op("pe", f_mm(ps[4], wu[s][:, 1, kt, :], h2T[:, kt, 640:1152], kt == 0, kt == 15), reads=[k, "h2T"], writes=["ps4"])
                gk = "GP0"
                P.op("dve", f_ts(gp[:, 0:8], ps[2][:, 0:8], self.cst[:, 23:24], None, ALU.mult), reads=["ps2", "cst"], writes=[gk])
                P.op("act", f_act(gp[:, 8:520], ps[0], ACTF.Identity), reads=["ps0", gk], writes=[gk])
                P.op("act", f_act(gp[:, 520:1032], ps[1], ACTF.Identity), reads=["ps1", gk], writes=[gk])
                w0, w1, w2 = cw[:, mt:mt + 1], cw[:, 44 + mt:45 + mt], cw[:, 88 + mt:89 + mt]
                ck = "cA0"
                P.op("dve", f_ts(ca, gp[:, 6:1030], w0, cbias[:, mt:mt + 1], ALU.mult, ALU.add), reads=[gk, "cw", "cbias"], writes=[ck])
                P.op("dve", f_stt(ca, gp[:, 7:1031], w1, ca, ALU.mult, ALU.add), reads=[gk, "cw", ck], writes=[ck])
                P.op("dve", f_stt(ca, gp[:, 8:1032], w2, ca, ALU.mult, ALU.add), reads=[gk, "cw", ck], writes=[ck])
                P.op("act", f_act(sb, ca, ACTF.Silu), reads=[ck], writes=["sB0"])
                P.op("dve", f_tt(ACTB[:, pr, 0:512], ps[3], sb[:, 0:512], ALU.mult), reads=["ps3", "sB0"], writes=["ACTB"])
                P.op("dve", f_tt(ACTB[:, pr, 512:1024], ps[4], sb[:, 512:1024], ALU.mult), reads=["ps4", "sB0"], writes=["ACTB"])
                if mt + 2 < 44:
                    uload(mt + 2)
            dload(ch, 0, dcnt % 2)
            for cb in range(4):
                s = dcnt % 2
                dcnt += 1
                if cb + 1 < 4:
                    dload(ch, cb + 1, 1 - s)
                for n in range(1, 9):
                    bank = 5 + n % 3
                    ts_ = n % 2
                    for kt in range(11):
                        P.op("pe", f_mm(ps[bank], ACTB[:, kt, (n - 1) * 128:n * 128], wd[s][:, kt, :], kt == 0, kt == 10),
                             reads=["ACTB", "wd%d" % s], writes=["ps%d" % bank])
                    dst = x1[:, n, cb * 512:(cb + 1) * 512]
                    P.op("dve", f_tt(tmp[ts_], ps[bank], g2bc[:, cb * 512:(cb + 1) * 512], ALU.mult), reads=["ps%d" % bank, "gbc"], writes=["tmp%d" % ts_])
                    P.op("dve", f_tt(dst, dst, tmp[ts_], ALU.add), reads=["x1_%d" % n, "tmp%d" % ts_], writes=["x1_%d" % n])
        self.dump("x2", x1, [128, 9, D], ["x1_%d" % n for n in range(9)])
        fgbc = g2bc
        P.op("sp", f_dma(fgbc, self.final_g.partition_broadcast(128)), reads=["gbc"], writes=["gbc"], chan=self.chan("fgbc"))
        junk = cA[0].bitcast(BF16)
        st_ = A.alloc((4,))
        for n in range(1, 9):
            i = n % 2
            ss, rs = st_[:, 2 * i:2 * i + 1], st_[:, 2 * i + 1:2 * i + 2]
            xk = "x1_%d" % n
            P.op("act", f_act(junk, x1[:, n, :], ACTF.Square, accum_out=ss), reads=[xk], writes=["cA0", "fss%d" % i])
            P.op("act", f_act(rs, ss, ACTF.Sqrt, bias=1e-6, scale=1.0 / D), reads=["fss%d" % i], writes=["frs%d" % i])
            P.op("dve", f_recip(rs, rs), reads=["frs%d" % i], writes=["frs%d" % i])
            P.op("dve", f_stt(x1[:, n, :], x1[:, n, :], rs, fgbc, ALU.mult, ALU.mult), reads=[xk, "frs%d" % i, "gbc"], writes=[xk])
            P.op("sp", f_dma(self.out[(n - 1) * 128:n * 128, :], x1[:, n, :]), reads=[xk], chan=self.chan("out"))

    def finish(self):
        self.P.emit(self.nc, self.st)
        self.st.close()
        return self.nc


def build(debug=(), stop_after=None):
    B = Builder(debug, stop_after)
    B.setup()
    gen = B.ssm_precompute()
    next(gen)
    for blk in range(24):
        B.mod_tiles(blk * 4, blk * 4 + 4)
        if blk >= 2:
            next(gen, None)
    for _ in gen:
        pass
    stages = [("setup_rest", "setup_rest"), ("setup_b", "setup_b"), ("phase_u", "phase_u"), ("ssm", "ssm_main"),
              ("mixer", "mixer"), ("ffn", "ffn")]
    for name, meth in stages:
        getattr(B, meth)()
        if stop_after == name:
            break
    nc = B.finish()
    return B, nc


def host_inputs(inp):
    f = lambda a: np.ascontiguousarray(np.asarray(a, dtype=np.float32))
    x = f(inp["x"])
    shared = {
        "ada_w": f(inp["ada_w"][0]), "ada_b": f(inp["ada_b"][0]), "g_mix": f(inp["norm_mix_g"][0]),
        "w_in": f(inp["w_in"][0]), "sinks": f(inp["attn_sinks"][0]), "w_ap": f(inp["w_attn_proj"][0]),
        "a_re": f(inp["ssm_a_re"][0]), "a_im": f(inp["ssm_a_im"][0]), "log_dt": f(inp["ssm_log_dt"][0]),
        "b_re": f(inp["ssm_b_re"][0]), "b_im": f(inp["ssm_b_im"][0]), "c_re": f(inp["ssm_c_re"][0]),
        "c_im": f(inp["ssm_c_im"][0]), "ssm_d": f(inp["ssm_d"][0]), "w_glu": f(inp["w_ssm_glu"][0]),
        "w_out": f(inp["w_out"][0]), "g_ffn": f(inp["norm_ffn_g"][0]), "w_up": f(inp["w_ffn_up"][0]),
        "conv_w": f(inp["ffn_conv_w"][0]), "conv_b": f(inp["ffn_conv_b"][0]), "w_down": f(inp["w_ffn_down"][0]),
        "final_g": f(inp["final_g"]),
    }
    maps = []
    p = np.arange(128, dtype=np.float32)
    for c in range(8):
        b, j = c // 4, c % 4
        start = 1024 * (j + 1) - 4096
        xw = np.zeros((4096, D), np.float32)
        lo = max(0, -start)
        xw[lo:] = x[b, start + lo:start + 4096]
        cst = np.zeros((128, 64), np.float32)
        for w in range(NW):
            cst[:, w] = 1.0 if start + 128 * w >= 0 else 0.0
        for e in range(10):
            cst[:, 32 + e] = 0.0 if start + 128 * (22 + e) >= 0 else NEG
        cst[:, 42] = 64.0 - p
        cst[:, 43] = p - 65.0
        cst[:, 44] = np.where(p < 64, 1.0, -1.0)
        m = dict(shared)
        m["xw"] = xw
        m["cb"] = f(inp["c"][b])
        m["cst"] = cst
        maps.append(m)
    return maps


def run_cores(inp, debug=(), stop_after=None, trace=False):
    B, nc = build(debug, stop_after)
    maps = host_inputs(inp)
    maps = [{k: m[k] for k in B.in_names} for m in maps]
    res = run_bass_kernel_spmd(nc, maps, core_ids=list(range(8)))
    return B, res


def kernel(**inputs):
    B, res = run_cores(inputs)
    out = np.zeros((2, 4096, D), np.float32)
    for c in range(8):
        b, j = c // 4, c % 4
        out[b, 1024 * j:1024 * (j + 1)] = np.asarray(res.results[c]["out"], dtype=np.float32)
    return out
```

```python
import os
import numpy as np
from contextlib import ExitStack
import concourse.bass as bass
import concourse.mybir as mybir
from concourse.bass_utils import run_bass_kernel_spmd

F32 = mybir.dt.float32
BF16 = mybir.dt.bfloat16
ALU = mybir.AluOpType
ACTF = mybir.ActivationFunctionType

D = 2048
KT = 16
NW = 32
TWO_PI = float(2 * np.pi)
MAGIC = 12582912.0
NEG = -30000.0
ENGS = ("pe", "dve", "act", "pool", "sp")


class Op:
    __slots__ = ("eng", "fn", "reads", "writes", "chan", "idx", "deps", "tick", "sig")

    def __init__(self, eng, fn, reads, writes, chan):
        self.eng, self.fn, self.reads, self.writes, self.chan = eng, fn, reads, writes, chan
        self.deps = {}
        self.tick = None
        self.sig = False


class Prog:
    def __init__(self):
        self.ops = []
        self.nchan = 0
        self.barriers = []

    def new_chan(self):
        self.nchan += 1
        return self.nchan - 1

    def op(self, eng, fn, reads=(), writes=(), chan=None):
        o = Op(eng, fn, tuple(reads), tuple(writes), chan)
        o.idx = len(self.ops)
        self.ops.append(o)
        return o

    def barrier(self):
        self.barriers.append(len(self.ops))

    def analyze(self):
        last_w, readers, last_sid, bar_last = {}, {}, {}, {}
        bars = set(self.barriers)
        for o in self.ops:
            if o.idx in bars:
                bar_last = dict(last_sid)
            cand = []
            for k in o.reads:
                w = last_w.get(k)
                if w is not None:
                    cand.append((w, "raw"))
            for k in o.writes:
                w = last_w.get(k)
                if w is not None:
                    cand.append((w, "waw"))
                for r in readers.get(k, ()):
                    cand.append((r, "war"))
            for p in bar_last.values():
                cand.append((p, "bar"))
            for (p, kind) in cand:
                if p is o:
                    continue
                if p.chan is None and o.chan is None and p.eng == o.eng:
                    if kind in ("war", "bar"):
                        continue
                    if kind == "waw" and o.eng == "pe":
                        continue
                sid = ("c", p.chan) if p.chan is not None else ("e", p.eng)
                cur = o.deps.get(sid)
                if cur is None or cur.idx < p.idx:
                    o.deps[sid] = p
            for k in o.reads:
                readers.setdefault(k, []).append(o)
            for k in o.writes:
                last_w[k] = o
                readers[k] = []
            last_sid[("c", o.chan) if o.chan is not None else ("e", o.eng)] = o
        for o in self.ops:
            for p in o.deps.values():
                p.sig = True
        ticks = {}
        for o in self.ops:
            if o.chan is not None:
                sid = ("c", o.chan)
                ticks[sid] = ticks.get(sid, 0) + 16
                o.tick = ticks[sid]
                o.sig = True
            elif o.sig:
                sid = ("e", o.eng)
                ticks[sid] = ticks.get(sid, 0) + 1
                o.tick = ticks[sid]
        self.final_ticks = ticks

    def emit(self, nc, st):
        self.analyze()
        sems = {}
        for e in ENGS:
            sems[("e", e)] = st.enter_context(nc.semaphore("s_" + e))
        for c in range(self.nchan):
            sems[("c", c)] = st.enter_context(nc.semaphore("c_%d" % c))
        block = st.enter_context(nc.Block())
        per_eng = {e: [o for o in self.ops if o.eng == e] for e in ENGS}
        final_ticks = self.final_ticks

        def run(eng_name, eng):
            known = {}
            for o in per_eng[eng_name]:
                for sid, p in o.deps.items():
                    if known.get(sid, 0) >= p.tick:
                        continue
                    eng.wait_ge(sems[sid], p.tick)
                    known[sid] = p.tick
                ins = o.fn(eng)
                if o.sig:
                    sid = ("c", o.chan) if o.chan is not None else ("e", o.eng)
                    ins.then_inc(sems[sid], 16 if o.chan is not None else 1)
            if eng_name == "sp":
                for sid, t in final_ticks.items():
                    eng.wait_ge(sems[sid], t)

        @block.tensor
        def _(eng):
            run("pe", eng)

        @block.vector
        def _(eng):
            run("dve", eng)

        @block.scalar
        def _(eng):
            run("act", eng)

        @block.gpsimd
        def _(eng):
            run("pool", eng)

        @block.sync
        def _(eng):
            run("sp", eng)


class Arena:
    def __init__(self, ap, nbytes):
        self.ap, self.nbytes, self.off, self.peak = ap, nbytes, 0, 0

    def alloc(self, shape, dtype=F32):
        n = int(np.prod(shape))
        sz = 4 if dtype == F32 else 2
        b = (n * sz + 31) // 32 * 32
        assert self.off + b <= self.nbytes, ("SBUF arena overflow", self.off, b, self.nbytes)
        v = self.ap[:, self.off // 4:(self.off + b) // 4]
        if dtype != F32:
            v = v.bitcast(dtype)
        v = v[:, 0:n]
        self.off += b
        self.peak = max(self.peak, self.off)
        if len(shape) == 2:
            v = v.rearrange("p (a b) -> p a b", b=shape[1])
        elif len(shape) == 3:
            v = v.rearrange("p (a b c) -> p a b c", b=shape[1], c=shape[2])
        elif len(shape) == 4:
            v = v.rearrange("p (a b c d) -> p a b c d", b=shape[1], c=shape[2], d=shape[3])
        return v

    def alloc_top(self, shape, dtype=F32):
        n = int(np.prod(shape))
        sz = 4 if dtype == F32 else 2
        b = (n * sz + 31) // 32 * 32
        assert self.off + b <= self.nbytes, ("SBUF arena overflow(top)", self.off, b, self.nbytes)
        self.nbytes -= b
        v = self.ap[:, self.nbytes // 4:(self.nbytes + b) // 4]
        if dtype != F32:
            v = v.bitcast(dtype)
        v = v[:, 0:n]
        if len(shape) == 2:
            v = v.rearrange("p (a b) -> p a b", b=shape[1])
        return v, b

    def free_top(self, b):
        self.nbytes += b

    def mark(self):
        return self.off

    def reset(self, m):
        self.off = m


def f_dma(out, in_):
    return lambda e: e.dma_start(out=out, in_=in_)


def f_mm(out, lhsT, rhs, start=True, stop=True, sgc=False):
    if sgc:
        return lambda e: e.matmul(out, lhsT, rhs, start=start, stop=stop, skip_group_check=True)
    return lambda e: e.matmul(out, lhsT, rhs, start=start, stop=stop)


def f_tr(out, in_, ident):
    return lambda e: e.transpose(out, in_, ident)


def f_act(out, in_, func, bias=None, scale=None, accum_out=None):
    kw = {}
    if bias is not None:
        kw["bias"] = bias
    if scale is not None:
        kw["scale"] = scale
    if accum_out is not None:
        kw["accum_out"] = accum_out
    return lambda e: e.activation(out, in_, func, **kw)


def f_ts(out, in0, s1, s2, op0, op1=None):
    if op1 is None:
        return lambda e: e.tensor_scalar(out, in0, s1, None, op0)
    return lambda e: e.tensor_scalar(out, in0, s1, s2, op0, op1)


def f_tt(out, in0, in1, op):
    return lambda e: e.tensor_tensor(out, in0, in1, op)


def f_stt(out, in0, scalar, in1, op0, op1):
    return lambda e: e.scalar_tensor_tensor(out, in0, scalar, in1, op0, op1)


def f_copy(out, in_):
    return lambda e: e.tensor_copy(out, in_)


def f_memset(ap, v):
    return lambda e: e.memset(ap, v)


def f_recip(out, in_):
    return lambda e: e.reciprocal(out, in_)


def f_asel(out, in_, cmp, fill, base, pattern, cm):
    return lambda e: e.affine_select(out=out, in_=in_, compare_op=cmp, fill=fill, base=base,
                                     pattern=pattern, channel_multiplier=cm)


class Builder:
    def __init__(self, debug=(), stop_after=None):
        self.nc = nc = bass.Bass("TRN2", target_bir_lowering=False)
        self.P = Prog()
        self.debug = set(debug)
        self.stop_after = stop_after
        self.dbg = {}
        self.st = ExitStack()
        self.in_names = []
        d = self.dram_in
        self.xw = d("xw", [4096, D])
        self.cb = d("cb", [D])
        self.cstd = d("cst", [128, 64])
        self.ada_w = d("ada_w", [D, 6 * D])
        self.ada_b = d("ada_b", [6 * D])
        self.g_mix = d("g_mix", [D])
        self.w_in = d("w_in", [D, 6400])
        self.sinks = d("sinks", [16])
        self.w_ap = d("w_ap", [1024, D])
        self.a_re = d("a_re", [64, 64])
        self.a_im = d("a_im", [64, 64])
        self.log_dt = d("log_dt", [64])
        self.b_re = d("b_re", [64, 64, 16])
        self.b_im = d("b_im", [64, 64, 16])
        self.c_re = d("c_re", [64, 16, 64])
        self.c_im = d("c_im", [64, 16, 64])
        self.ssm_d = d("ssm_d", [64, 16])
        self.w_glu = d("w_glu", [1024, 2 * D])
        self.w_out = d("w_out", [D, D])
        self.g_ffn = d("g_ffn", [D])
        self.w_up = d("w_up", [D, 11264])
        self.conv_w = d("conv_w", [3, 5632])
        self.conv_b = d("conv_b", [5632])
        self.w_down = d("w_down", [5632, D])
        self.final_g = d("final_g", [D])
        self.out = nc.dram_tensor("out", [1024, D], F32, kind="ExternalOutput").ap()
        self.Wd = nc.dram_tensor("Wd", [64, 8, 2, 16, 64], BF16, kind="Internal").ap()
        self.abd = nc.dram_tensor("abd", [2, 64, 64], F32, kind="Internal").ap()
        self.rowd = nc.dram_tensor("rowd", [6, 64, 64], F32, kind="Internal").ap()
        W = 212000 // 4
        sb = self.st.enter_context(nc.sbuf_tensor("arena", [128, W], F32))
        self.A = Arena(sb[:], W * 4)
        self.ps = [self.st.enter_context(nc.psum_tensor("psb%d" % i, [128, 512], F32))[:] for i in range(8)]
        self.chan_cache = {}

    def dram_in(self, name, shape):
        if self.stop_after in ("setup", "ssm_pre", "phase_u", "ssm") and name in ("w_ap", "w_glu", "w_out", "w_up", "w_down"):
            return None
        self.in_names.append(name)
        return self.nc.dram_tensor(name, shape, F32, kind="ExternalInput").ap()

    def chan(self, name):
        if name not in self.chan_cache:
            self.chan_cache[name] = self.P.new_chan()
        return self.chan_cache[name]

    def dump(self, name, ap, shape, reads):
        if name not in self.debug:
            return
        dt = ap.dtype
        t = self.nc.dram_tensor("dbg_" + name, list(shape), dt, kind="ExternalOutput").ap()
        self.dbg[name] = "dbg_" + name
        self.P.op("sp", f_dma(t, ap), reads=reads, chan=self.chan("dbg"))

    def setup(self):
        A, P = self.A, self.P
        cst = self.cst = A.alloc((64,))
        P.op("sp", f_dma(cst, self.cstd), writes=["cst"], chan=self.chan("cst"))
        identF = self.identF = A.alloc((128,))
        identB = self.identB = A.alloc((128,), BF16)
        onesF = self.onesF = A.alloc((128,))
        onesB = self.onesB = A.alloc((128,), BF16)
        Ls = self.Lstrict = A.alloc((128,))
        P.op("dve", f_memset(identF, 0.0), writes=["identF"])
        P.op("pool", f_asel(identF, identF, ALU.not_equal, 1.0, 0, [[-1, 128]], 1), reads=["identF"], writes=["identF"])
        P.op("dve", f_copy(identB, identF), reads=["identF"], writes=["identB"])
        P.op("dve", f_memset(onesF, 1.0), writes=["onesF"])
        P.op("dve", f_memset(onesB, 1.0), writes=["onesB"])
        P.op("pool", f_asel(Ls, onesF, ALU.is_gt, 0.0, 0, [[1, 128]], -1), reads=["onesF"], writes=["Ls"])
        self.maskcat = A.alloc((256,), BF16)
        self.mask01 = A.alloc((256,), BF16)
        self.diag = A.alloc((2, 128))
        self.onespad = A.alloc((2, 128), BF16)
        P.op("dve", f_memset(self.onespad, 0.0), writes=["onespad"])
        P.op("dve", f_memset(self.onespad[:, 0, 0:64], 1.0), reads=["onespad"], writes=["onespad"])
        P.op("dve", f_memset(self.onespad[:, 1, 64:128], 1.0), reads=["onespad"], writes=["onespad"])
        gm = self.gmask = A.alloc((8,))
        P.op("dve", f_memset(gm, 1.0), writes=["gmask"])
        P.op("pool", f_asel(gm, gm, ALU.is_ge, 0.0, 0, [[-16, 8]], 1), reads=["gmask"], writes=["gmask"])
        P.op("pool", f_asel(gm, gm, ALU.is_ge, 0.0, 15, [[16, 8]], -1), reads=["gmask"], writes=["gmask"])
        m0 = A.mark()
        rows = self.rows = A.alloc((128,))
        self.MOD = A.alloc((96,))
        self.vecs = A.alloc((3, 16))
        self.adab = A.alloc((96,))
        self.condb = A.alloc((16,), BF16)
        ps = self.ps

        def to_fp(src_rows_ap, nrows, dst, key, tag):
            P.op("sp", f_dma(rows[0:nrows, :], src_rows_ap), writes=["rows"], chan=self.chan("rows"))
            P.op("pe", f_tr(ps[0][:, 0:nrows], rows[0:nrows, :], identF[0:nrows, 0:nrows]), reads=["rows", "identF"], writes=["ps0"])
            P.op("dve", f_copy(dst, ps[0][:, 0:nrows]), reads=["ps0"], writes=[key])

        to_fp(self.cb.rearrange("(a b) -> a b", b=128), 16, self.vecs[:, 0, :], "vec0", "c")
        to_fp(self.g_mix.rearrange("(a b) -> a b", b=128), 16, self.vecs[:, 1, :], "vec1", "gm")
        to_fp(self.g_ffn.rearrange("(a b) -> a b", b=128), 16, self.vecs[:, 2, :], "vec2", "gf")
        to_fp(self.ada_b.rearrange("(a b) -> a b", b=128), 96, self.adab, "adab", "ab")
        P.op("act", f_act(self.condb, self.vecs[:, 0, :], ACTF.Silu), reads=["vec0"], writes=["condb"])
        self.sp = A.alloc((2, 16))
        self.res0_mark = A.mark()
        self.Wc = A.alloc((8, 8, 2, 64), BF16)
        self.Kc = A.alloc((8, 15, 16), BF16)
        self.WoutK = A.alloc((64, 9, 16), BF16)
        self.Dcol = A.alloc((8,))
        self.RT = A.alloc((8, 6, 4))
        self.wt_mark = A.mark()
        mk = A.alloc((256,))
        P.op("dve", f_memset(mk, 0.0), writes=["mk"])
        P.op("pool", f_asel(mk[:, 0:128], mk[:, 0:128], ALU.is_ge, NEG, 0, [[1, 128]], -1), reads=["mk"], writes=["mk"])
        P.op("pool", f_asel(mk[:, 128:256], mk[:, 128:256], ALU.is_gt, NEG, 0, [[-1, 128]], 1), reads=["mk"], writes=["mk"])
        P.op("dve", f_copy(self.maskcat, mk), reads=["mk"], writes=["maskcat"])
        m01 = A.alloc((256,))
        P.op("dve", f_memset(m01, 1.0), writes=["m01"])
        P.op("pool", f_asel(m01[:, 0:128], m01[:, 0:128], ALU.is_ge, 0.0, 0, [[1, 128]], -1), reads=["m01"], writes=["m01"])
        P.op("pool", f_asel(m01[:, 128:256], m01[:, 128:256], ALU.is_gt, 0.0, 0, [[-1, 128]], 1), reads=["m01"], writes=["m01"])
        P.op("dve", f_copy(self.mask01, m01), reads=["m01"], writes=["mask01"])
        cm = self.colmask = A.alloc((8, 128))
        P.op("dve", f_memset(cm, 1.0), writes=["colmask"])
        P.op("pool", f_asel(cm, cm, ALU.is_ge, 0.0, 0, [[-16, 8], [1, 128]], 0), reads=["colmask"], writes=["colmask"])
        P.op("pool", f_asel(cm, cm, ALU.is_ge, 0.0, 15, [[16, 8], [-1, 128]], 0), reads=["colmask"], writes=["colmask"])
        NS = 2
        wt = [A.alloc((16, 512), BF16) for _ in range(NS)]
        aw = self.ada_w.rearrange("(kt p) m -> p kt m", p=128)

        def load(blk):
            s = blk % NS
            P.op("pool", f_dma(wt[s], aw[:, :, blk * 512:(blk + 1) * 512]), writes=["adaw%d" % s], chan=self.chan("adaw%d" % s))

        for blk in range(NS):
            load(blk)

        rowb = [A.alloc((512,)) for _ in range(2)]
        one1 = self.onesF[0:1, 0:1]

        def mod_tiles(lo, hi):
            for blk in range(lo // 4, hi // 4):
                s = blk % NS
                for kt in range(KT):
                    P.op("pe", f_mm(ps[0][0:1, :], self.condb[:, kt:kt + 1], wt[s][:, kt, :], kt == 0, kt == KT - 1),
                         reads=["adaw%d" % s, "condb"], writes=["ps0"])
                rb = rowb[blk % 2]
                P.op("act", f_act(rb[0:1], ps[0][0:1, :], ACTF.Identity), reads=["ps0"], writes=["rowb%d" % (blk % 2)])
                for c in range(4):
                    mt = blk * 4 + c
                    P.op("pe", f_mm(ps[1][:, mt:mt + 1], rb[0:1, c * 128:(c + 1) * 128], one1, True, True, True),
                         reads=["rowb%d" % (blk % 2), "onesF"], writes=["ps1"])
                if blk + NS < 24:
                    load(blk + NS)

        self.mod_tiles = mod_tiles

    def setup_rest(self):
        self.P.barrier()

    def setup_b(self):
        P, ps = self.P, self.ps
        P.op("dve", f_tt(self.MOD, ps[1][:, 0:96], self.adab, ALU.add), reads=["ps1", "adab"], writes=["MOD"])
        P.op("dve", f_stt(self.sp[:, 0, :], self.MOD[:, 16:32], 1.0, self.vecs[:, 1, :], ALU.add, ALU.mult), reads=["MOD", "vec1"], writes=["sp"])
        P.op("dve", f_stt(self.sp[:, 1, :], self.MOD[:, 64:80], 1.0, self.vecs[:, 2, :], ALU.add, ALU.mult), reads=["MOD", "vec2", "sp"], writes=["sp"])
        self.dump("MOD", self.MOD, [128, 96], ["MOD"])
        self.dump("sp", self.sp, [128, 2, 16], ["sp"])

    def late_consts(self):
        A, P, ps = self.A, self.P, self.ps
        identF, onesF = self.identF, self.onesF
        self.gbc = A.alloc((2, D))
        self.fgbc = A.alloc((D,))
        diag = A.alloc((2, 128))
        for gi, c0 in ((0, 32), (1, 80)):
            for kt in range(KT):
                b = kt % 2
                P.op("dve", f_ts(diag[:, b, :], identF, self.MOD[:, c0 + kt:c0 + kt + 1], None, ALU.mult), reads=["identF", "MOD"], writes=["diag%d" % b])
                bank = 2 + kt // 4
                P.op("pe", f_mm(ps[bank][:, (kt % 4) * 128:(kt % 4 + 1) * 128], onesF, diag[:, b, :]), reads=["onesF", "diag%d" % b], writes=["ps%d" % bank])
            for q in range(4):
                P.op("act", f_act(self.gbc[:, gi, q * 512:(q + 1) * 512], ps[2 + q], ACTF.Identity), reads=["ps%d" % (2 + q)], writes=["gbc"])
        P.op("sp", f_dma(self.fgbc, self.final_g.partition_broadcast(128)), writes=["fgbc"], chan=self.chan("fgbc"))
        self.dump("gbc", self.gbc, [128, 2, D], ["gbc"])

    def wrap_sincos(self, eng, phi, temps, key, want):
        P = self.P
        ti = 0
        for (outap, kind, okey) in want:
            src, skey = phi, key
            if kind == "cos":
                psi, pk = temps[ti]; ti += 1
                P.op(eng, f_ts(psi, phi, float(np.pi / 2), None, ALU.add), reads=[key], writes=[pk])
                src, skey = psi, pk
            k1, kk = temps[ti]; ti += 1
            P.op(eng, f_ts(k1, src, 1.0 / TWO_PI, MAGIC, ALU.mult, ALU.add), reads=[skey], writes=[kk])
            P.op(eng, f_ts(k1, k1, -MAGIC, -TWO_PI, ALU.add, ALU.mult), reads=[kk], writes=[kk])
            P.op(eng, f_tt(k1, src, k1, ALU.add), reads=[skey, kk], writes=[kk])
            P.op("act", f_act(outap, k1, ACTF.Sin), reads=[kk], writes=[okey])

    def ssm_precompute(self):
        A, P, ps = self.A, self.P, self.ps
        identF, identB = self.identF, self.identB
        res_mark = self.wt_mark
        G = 64
        ar = A.alloc((64,)); ai = A.alloc((64,)); ldt = A.alloc((1,)); dt = A.alloc((1,))
        P.op("sp", f_dma(ar[0:G], self.a_re), writes=["ar"], chan=self.chan("ssmld1"))
        P.op("sp", f_dma(ai[0:G], self.a_im), writes=["ai"], chan=self.chan("ssmld2"))
        P.op("sp", f_dma(ldt[0:G], self.log_dt.rearrange("(g o) -> g o", o=1)), writes=["ldt"], chan=self.chan("ssmld3"))
        P.op("act", f_act(dt[0:G], ldt[0:G], ACTF.Exp), reads=["ldt"], writes=["dt"])
        ardt = A.alloc((64,)); aidt = A.alloc((64,))
        P.op("dve", f_ts(ardt[0:G], ar[0:G], dt[0:G], None, ALU.mult), reads=["ar", "dt"], writes=["ardt"])
        P.op("dve", f_ts(aidt[0:G], ai[0:G], dt[0:G], None, ALU.mult), reads=["ai", "dt"], writes=["aidt"])
        ks = [0, 1, 2, 3, 4, 5, 6, 7, 8, 504, 1024, 520]
        NK = len(ks)
        Lr = A.alloc((NK, 64)); Li = A.alloc((NK, 64)); mag = A.alloc((NK, 64)); ang = A.alloc((NK, 64))
        for i, k in enumerate(ks):
            P.op("act", f_act(mag[0:G, i, :], ardt[0:G], ACTF.Exp, scale=float(k)), reads=["ardt"], writes=["mag"])
            P.op("dve", f_ts(ang[0:G, i, :], aidt[0:G], float(k), None, ALU.mult), reads=["aidt"], writes=["ang"])
        sn = A.alloc((NK, 64)); cs = A.alloc((NK, 64))
        tmps = [(A.alloc((NK, 64))[0:G], "sctmp%d" % i) for i in range(3)]
        self.wrap_sincos("dve", ang[0:G], tmps, "ang", [(sn[0:G], "sin", "sn"), (cs[0:G], "cos", "cs")])
        P.op("dve", f_tt(Lr[0:G], mag[0:G], cs[0:G], ALU.mult), reads=["mag", "cs"], writes=["Lr"])
        P.op("dve", f_tt(Li[0:G], mag[0:G], sn[0:G], ALU.mult), reads=["mag", "sn"], writes=["Li"])
        den = A.alloc((64,)); t1 = A.alloc((64,)); t2 = A.alloc((64,)); lm1 = A.alloc((64,)); zr = A.alloc((64,)); zi = A.alloc((64,))
        P.op("dve", f_tt(den[0:G], ar[0:G], ar[0:G], ALU.mult), reads=["ar"], writes=["den"])
        P.op("dve", f_tt(t1[0:G], ai[0:G], ai[0:G], ALU.mult), reads=["ai"], writes=["t1"])
        P.op("dve", f_tt(den[0:G], den[0:G], t1[0:G], ALU.add), reads=["den", "t1"], writes=["den"])
        P.op("dve", f_recip(den[0:G], den[0:G]), reads=["den"], writes=["den"])
        P.op("dve", f_ts(lm1[0:G], Lr[0:G, 1, :], -1.0, None, ALU.add), reads=["Lr"], writes=["lm1"])
        P.op("dve", f_tt(t1[0:G], lm1[0:G], ar[0:G], ALU.mult), reads=["lm1", "ar"], writes=["t1"])
        P.op("dve", f_tt(t2[0:G], Li[0:G, 1, :], ai[0:G], ALU.mult), reads=["Li", "ai"], writes=["t2"])
        P.op("dve", f_tt(t1[0:G], t1[0:G], t2[0:G], ALU.add), reads=["t1", "t2"], writes=["t1"])
        P.op("dve", f_tt(zr[0:G], t1[0:G], den[0:G], ALU.mult), reads=["t1", "den"], writes=["zr"])
        P.op("dve", f_tt(t1[0:G], Li[0:G, 1, :], ar[0:G], ALU.mult), reads=["Li", "ar"], writes=["t1"])
        P.op("dve", f_tt(t2[0:G], lm1[0:G], ai[0:G], ALU.mult), reads=["lm1", "ai"], writes=["t2"])
        P.op("dve", f_tt(t1[0:G], t1[0:G], t2[0:G], ALU.subtract), reads=["t1", "t2"], writes=["t1"])
        P.op("dve", f_tt(zi[0:G], t1[0:G], den[0:G], ALU.mult), reads=["t1", "den"], writes=["zi"])
        br = A.alloc((64, 16)); bi = A.alloc((64, 16))
        P.op("sp", f_dma(br[0:G], self.b_re), writes=["br"], chan=self.chan("ssmld4"))
        P.op("sp", f_dma(bi[0:G], self.b_im), writes=["bi"], chan=self.chan("ssmld5"))
        brT = br[0:G].rearrange("g n q -> g q n"); biT = bi[0:G].rearrange("g n q -> g q n")
        Bb = A.alloc((2, 16, 64)); tq = A.alloc((16, 64)); tq2 = A.alloc((16, 64))

        def bc_n(v):
            return v.unsqueeze(1).to_broadcast([G, 16, 64])

        P.op("dve", f_tt(tq[0:G], brT, bc_n(zr[0:G]), ALU.mult), reads=["br", "zr"], writes=["tq"])
        P.op("dve", f_tt(tq2[0:G], biT, bc_n(zi[0:G]), ALU.mult), reads=["bi", "zi"], writes=["tq2"])
        P.op("dve", f_tt(Bb[0:G, 0], tq[0:G], tq2[0:G], ALU.subtract), reads=["tq", "tq2"], writes=["Bb0"])
        P.op("dve", f_tt(tq[0:G], biT, bc_n(zr[0:G]), ALU.mult), reads=["bi", "zr", "Bb0"], writes=["tq"])
        P.op("dve", f_tt(tq2[0:G], brT, bc_n(zi[0:G]), ALU.mult), reads=["br", "zi", "Bb0"], writes=["tq2"])
        P.op("dve", f_tt(Bb[0:G, 1], tq[0:G], tq2[0:G], ALU.add), reads=["tq", "tq2"], writes=["Bb1"])
        Ws = A.alloc((2, 2, 16, 64), BF16)
        for s in range(8):
            k = 7 - s
            sl = s % 2
            lr, li = bc_n(Lr[0:G, k, :]), bc_n(Li[0:G, k, :])
            P.op("dve", f_tt(tq[0:G], Bb[0:G, 0], lr, ALU.mult), reads=["Bb0", "Lr"], writes=["tq"])
            P.op("dve", f_tt(tq2[0:G], Bb[0:G, 1], li, ALU.mult), reads=["Bb1", "Li"], writes=["tq2"])
            P.op("dve", f_tt(Ws[0:G, sl, 0], tq[0:G], tq2[0:G], ALU.subtract), reads=["tq", "tq2"], writes=["Ws%d" % sl])
            P.op("dve", f_tt(tq[0:G], Bb[0:G, 1], lr, ALU.mult), reads=["Bb1", "Lr", "Ws%d" % sl], writes=["tq"])
            P.op("dve", f_tt(tq2[0:G], Bb[0:G, 0], li, ALU.mult), reads=["Bb0", "Li", "Ws%d" % sl], writes=["tq2"])
            P.op("dve", f_tt(Ws[0:G, sl, 1], tq[0:G], tq2[0:G], ALU.add), reads=["tq", "tq2", "Ws%d" % sl], writes=["Ws%d" % sl])
            P.op("sp", f_dma(self.Wd[:, s], Ws[0:G, sl]), reads=["Ws%d" % sl], writes=["Wd"], chan=self.chan("wd%d" % sl))
        ab = A.alloc((2, 64))
        P.op("dve", f_ts(ab[0:G, 0, :], ardt[0:G], 8.0, None, ALU.mult), reads=["ardt"], writes=["ab"])
        P.op("dve", f_ts(ab[0:G, 1, :], aidt[0:G], 8.0, None, ALU.mult), reads=["aidt", "ab"], writes=["ab"])
        P.op("sp", f_dma(self.abd.rearrange("a g n -> g a n"), ab[0:G]), reads=["ab"], writes=["abd"], chan=self.chan("abd"))
        rt = A.alloc((6, 64))
        for i, idx in enumerate((9, 10, 11)):
            P.op("dve", f_copy(rt[0:G, 2 * i, :], Lr[0:G, idx, :]), reads=["Lr"], writes=["rt"])
            P.op("dve", f_copy(rt[0:G, 2 * i + 1, :], Li[0:G, idx, :]), reads=["Li", "rt"], writes=["rt"])
        rt2 = A.alloc((6, 128))
        P.op("dve", f_copy(rt2[0:G, :, 0:64], rt[0:G]), reads=["rt"], writes=["rt2"])
        P.op("dve", f_copy(rt2[0:G, :, 64:128], rt[0:G]), reads=["rt", "rt2"], writes=["rt2"])
        for comp in range(6):
            P.op("pe", f_tr(ps[6][:, 0:G], rt2[0:G, comp, :], identF[0:G, 0:G]), reads=["rt2", "identF"], writes=["ps6"])
            v = ps[6][:, 0:G].rearrange("p (gt q e) -> p gt q e", q=4, e=2)
            P.op("dve", f_copy(self.RT[0:64, :, comp, :], v[0:64, :, :, 0]), reads=["ps6"], writes=["RT"])
            P.op("dve", f_copy(self.RT[64:128, :, comp, :], v[64:128, :, :, 1]), reads=["ps6", "RT"], writes=["RT"])
        yield
        for gl in range(8):
            for gt in range(8):
                src = self.Wd[gt * 8 + gl].rearrange("s c q n -> q (s c) n")
                dst = self.Wc[gl * 16:(gl + 1) * 16, gt].rearrange("p s c n -> p (s c) n")
                P.op("sp", f_dma(dst, src), reads=["Wd"], writes=["Wc"], chan=self.chan("wc"))
        LAn = A.alloc((9, 128)); LBn = A.alloc((9, 128))
        P.op("dve", f_copy(LAn[0:G, :, 0:64], Lr[0:G, 0:9, :]), reads=["Lr"], writes=["LAn"])
        P.op("dve", f_ts(LAn[0:G, :, 64:128], Lr[0:G, 0:9, :], -1.0, None, ALU.mult), reads=["Lr", "LAn"], writes=["LAn"])
        P.op("dve", f_copy(LBn[0:G, :, 0:64], Li[0:G, 0:9, :]), reads=["Li"], writes=["LBn"])
        P.op("dve", f_copy(LBn[0:G, :, 64:128], Li[0:G, 0:9, :]), reads=["Li", "LBn"], writes=["LBn"])
        LA = A.alloc((9, 64)); LB = A.alloc((9, 64))
        for k in range(9):
            for (src, dst, key, bank) in ((LAn, LA, "LA", 0), (LBn, LB, "LB", 7)):
                P.op("pe", f_tr(ps[bank][:, 0:G], src[0:G, k, :], identF[0:G, 0:G]), reads=[key + "n", "identF"], writes=["ps%d" % bank])
                P.op("act", f_act(dst[:, k, :], ps[bank][:, 0:G], ACTF.Identity), reads=["ps%d" % bank], writes=[key])
            yield
        Cab = A.alloc((8, 128)); Cba = A.alloc((8, 128))
        cre = self.c_re.rearrange("(gt gl) p n -> (gl p) gt n", gl=8)
        cim = self.c_im.rearrange("(gt gl) p n -> (gl p) gt n", gl=8)
        P.op("sp", f_dma(Cab[:, :, 0:64], cre), writes=["Cab"], chan=self.chan("ssmld6"))
        P.op("sp", f_dma(Cab[:, :, 64:128], cim), writes=["Cab"], chan=self.chan("ssmld7"))
        P.op("sp", f_dma(Cba[:, :, 0:64], cim), writes=["Cba"], chan=self.chan("ssmld8"))
        P.op("sp", f_dma(Cba[:, :, 64:128], cre), writes=["Cba"], chan=self.chan("ssmld9"))
        CTa = A.alloc((8, 128)); CTb = A.alloc((8, 128))
        for gt in range(8):
            for (src, dst, key, bank) in ((Cab, CTa, "CTa", 2), (Cba, CTb, "CTb", 3)):
                P.op("pe", f_tr(ps[bank][:, 0:128], src[:, gt, :], identF), reads=["Cab" if key == "CTa" else "Cba", "identF"], writes=["ps%d" % bank])
                P.op("act", f_act(dst[:, gt, :], ps[bank][:, 0:128], ACTF.Identity), reads=["ps%d" % bank], writes=[key])
            yield
        CTa4 = CTa.rearrange("m gt (gl p) -> m (gt gl) p", p=16)
        CTb4 = CTb.rearrange("m gt (gl p) -> m (gt gl) p", p=16)
        wt1 = A.alloc((64, 16)); wt2 = A.alloc((64, 16))
        for k in range(9):
            la = LA[:, k, :].unsqueeze(2).to_broadcast([128, 64, 16])
            lb = LB[:, k, :].unsqueeze(2).to_broadcast([128, 64, 16])
            P.op("dve", f_tt(wt1, CTa4, la, ALU.mult), reads=["CTa", "LA", "WoutK"], writes=["wt1"])
            P.op("dve", f_tt(wt2, CTb4, lb, ALU.mult), reads=["CTb", "LB", "WoutK"], writes=["wt2"])
            P.op("dve", f_tt(self.WoutK[:, :, k, :], wt1, wt2, ALU.subtract), reads=["wt1", "wt2"], writes=["WoutK"])
        P.op("dve", f_memset(self.Kc, 0.0), writes=["Kc"])
        BT = A.alloc((128,), BF16); BTp = A.alloc((8, 128), BF16)
        psb = ps[4].bitcast(BF16)
        for gt in range(8):
            P.op("pe", f_tr(psb[:, 0:128], self.Wc[:, gt, 7].rearrange("p c n -> p (c n)"), identB), reads=["Wc", "identB"], writes=["ps4"])
            P.op("act", f_act(BT, psb[:, 0:128], ACTF.Identity), reads=["ps4"], writes=["BT"])
            P.op("dve", f_tt(BTp, BT.unsqueeze(1).to_broadcast([128, 8, 128]), self.colmask, ALU.mult), reads=["BT", "colmask"], writes=["BTp"])
            for gl in range(8):
                g = gt * 8 + gl
                P.op("pe", f_mm(ps[5][:, 0:128], BTp[:, gl, :], self.WoutK[:, g, 0:8, :].rearrange("m k p -> m (k p)"), gl == 0, gl == 7),
                     reads=["BTp", "WoutK"], writes=["ps5"])
            P.op("act", f_act(self.Kc[:, gt, 7:15, :].rearrange("p k q -> p (k q)"), ps[5][:, 0:128], ACTF.Identity), reads=["ps5"], writes=["Kc"])
            yield
        P.op("sp", f_dma(self.rows[0:8, :], self.ssm_d.rearrange("(gt gl) p -> gt (gl p)", gl=8)), writes=["rows"], chan=self.chan("rows"))
        P.op("pe", f_tr(ps[6][:, 0:8], self.rows[0:8, :], identF[0:8, 0:8]), reads=["rows", "identF"], writes=["ps6"])
        P.op("dve", f_copy(self.Dcol, ps[6][:, 0:8]), reads=["ps6"], writes=["Dcol"])
        self.dump("Wc", self.Wc, [128, 8, 8, 2, 64], ["Wc"])
        self.dump("Kc", self.Kc, [128, 8, 8, 16], ["Kc"])
        self.dump("WoutK", self.WoutK, [128, 64, 9, 16], ["WoutK"])
        P.barrier()
        A.reset(res_mark)

    def norm_bufs(self, with_bias):
        A = self.A
        self.nb_xn = [A.alloc((D,), BF16) for _ in range(2)]
        self.nb_tmp = A.alloc((D,)) if with_bias else None
        self.nb_ss = A.alloc((4,))
        self.nb_cnt = 0

    def norm_to_fm(self, src, src_key, dst_fn, dst_key, scale_bc, sck, bias_bc=None, bk=None):
        P, ps = self.P, self.ps
        i = self.nb_cnt % 2
        self.nb_cnt += 1
        ss = self.nb_ss[:, 2 * i:2 * i + 1]
        rs = self.nb_ss[:, 2 * i + 1:2 * i + 2]
        xn = self.nb_xn[i]
        xk = "nbxn%d" % i
        P.op("act", f_act(xn, src, ACTF.Square, accum_out=ss), reads=[src_key], writes=[xk, "nbss%d" % i])
        P.op("act", f_act(rs, ss, ACTF.Sqrt, bias=1e-6, scale=1.0 / D), reads=["nbss%d" % i], writes=["nbrs%d" % i])
        P.op("dve", f_recip(rs, rs), reads=["nbrs%d" % i], writes=["nbrs%d" % i])
        if bias_bc is None:
            P.op("dve", f_stt(xn, src, rs, scale_bc, ALU.mult, ALU.mult), reads=[src_key, "nbrs%d" % i, sck], writes=[xk])
        else:
            P.op("dve", f_stt(self.nb_tmp, src, rs, scale_bc, ALU.mult, ALU.mult), reads=[src_key, "nbrs%d" % i, sck], writes=["nbtmp"])
            P.op("dve", f_tt(xn, self.nb_tmp, bias_bc, ALU.add), reads=["nbtmp", bk], writes=[xk])
        for half in range(2):
            bank = 6 + half
            psb = ps[bank].bitcast(BF16)
            for q in range(8):
                kt = half * 8 + q
                P.op("pe", f_tr(psb[:, q * 128:(q + 1) * 128], xn[:, kt * 128:(kt + 1) * 128], self.identB),
                     reads=[xk, "identB"], writes=["ps%d" % bank])
            src3 = psb.rearrange("p (a b) -> p a b", b=128)
            if half == 0:
                P.op("act", f_act(dst_fn(half), src3, ACTF.Identity), reads=["ps%d" % bank], writes=[dst_key])
            else:
                P.op("dve", f_copy(dst_fn(half), src3), reads=["ps%d" % bank], writes=[dst_key])

    def phase_u(self):
        A, P, ps = self.A, self.P, self.ps
        self.GY, self.GY_bytes = A.alloc_top((8, 1152), BF16)
        self.uT, self.uT_bytes = A.alloc_top((8, 4096), BF16)
        self.after_u_mark = A.mark()
        Wu = A.alloc((16, 1024), BF16)
        P.op("pool", f_dma(Wu, self.w_in.rearrange("(kt p) m -> p kt m", p=128)[:, :, 1280:2304]), writes=["Wu"], chan=self.chan("Wu"))
        s1bc = A.alloc((D,), BF16)
        self.gbc_build(s1bc, 0, vec=self.sp[:, 0, :], key="s1bc")
        sh1b = A.alloc((16,), BF16)
        cu = A.alloc((8,))
        P.op("dve", f_copy(sh1b, self.MOD[:, 0:16]), reads=["MOD"], writes=["sh1b"])
        for m in range(8):
            for kt in range(KT):
                P.op("pe", f_mm(ps[0][:, m:m + 1], Wu[:, kt, m * 128:(m + 1) * 128], sh1b[:, kt:kt + 1], kt == 0, kt == KT - 1, True),
                     reads=["Wu", "sh1b"], writes=["ps0"])
        P.op("dve", f_copy(cu, ps[0][:, 0:8]), reads=["ps0"], writes=["cu"])
        self.norm_bufs(False)
        xt = [A.alloc((D,)) for _ in range(2)]
        hTu = [A.alloc((16, 384), BF16), A.alloc((16, 256), BF16)]
        groups = []
        w_ = 0
        while w_ < NW:
            nt_ = min(3 if len(groups) % 2 == 0 else 2, NW - w_)
            groups.append((w_, nt_))
            w_ += nt_
        for gi, (w0, nt) in enumerate(groups):
            hs = gi % 2
            N = nt * 128
            for ti in range(nt):
                w = w0 + ti
                xs = w % 2
                P.op("sp", f_dma(xt[xs], self.xw[w * 128:(w + 1) * 128, :]), writes=["xt%d" % xs], chan=self.chan("xt%d" % xs))
                self.norm_to_fm(xt[xs], "xt%d" % xs, lambda half, hs=hs, ti=ti: hTu[hs][:, half * 8:(half + 1) * 8, ti * 128:(ti + 1) * 128],
                                "hTu%d" % hs, s1bc, "s1bc")
            mask = self.cst[:, w0:w0 + nt].unsqueeze(2).to_broadcast([128, nt, 128])
            for m in range(8):
                bank = 1 + m % 4
                for kt in range(KT):
                    P.op("pe", f_mm(ps[bank][:, 0:N], Wu[:, kt, m * 128:(m + 1) * 128], hTu[hs][:, kt, 0:N], kt == 0, kt == KT - 1),
                         reads=["Wu", "hTu%d" % hs], writes=["ps%d" % bank])
                dst = self.uT[:, m, w0 * 128:w0 * 128 + N].rearrange("p (a b) -> p a b", b=128)
                P.op("dve", f_stt(dst, ps[bank][:, 0:N].rearrange("p (a b) -> p a b", b=128), cu[:, m:m + 1], mask, ALU.add, ALU.mult),
                     reads=["ps%d" % bank, "cst", "cu"], writes=["uT"])
        self.dump("uT", self.uT, [128, 8, 4096], ["uT"])
        P.barrier()
        A.reset(self.after_u_mark)

    def cmul_cols(self, outr, outi, ko, ar, ai, br, bi, t1, t2, keys_r, accumulate=False):
        P = self.P
        e = "dve"
        if not accumulate:
            P.op(e, f_tt(outr, ar, br, ALU.mult), reads=keys_r, writes=[ko])
            P.op(e, f_tt(t1, ai, bi, ALU.mult), reads=keys_r, writes=["cc_t1"])
            P.op(e, f_tt(outr, outr, t1, ALU.subtract), reads=[ko, "cc_t1"], writes=[ko])
            P.op(e, f_tt(outi, ar, bi, ALU.mult), reads=keys_r + [ko], writes=[ko])
            P.op(e, f_tt(t2, ai, br, ALU.mult), reads=keys_r, writes=["cc_t2"])
            P.op(e, f_tt(outi, outi, t2, ALU.add), reads=[ko, "cc_t2"], writes=[ko])
        else:
            P.op(e, f_tt(t1, ar, br, ALU.mult), reads=keys_r, writes=["cc_t1"])
            P.op(e, f_tt(outr, outr, t1, ALU.add), reads=[ko, "cc_t1"], writes=[ko])
            P.op(e, f_tt(t1, ai, bi, ALU.mult), reads=keys_r + [ko], writes=["cc_t1"])
            P.op(e, f_tt(outr, outr, t1, ALU.subtract), reads=[ko, "cc_t1"], writes=[ko])
            P.op(e, f_tt(t2, ar, bi, ALU.mult), reads=keys_r, writes=["cc_t2"])
            P.op(e, f_tt(outi, outi, t2, ALU.add), reads=[ko, "cc_t2"], writes=[ko])
            P.op(e, f_tt(t2, ai, br, ALU.mult), reads=keys_r + [ko], writes=["cc_t2"])
            P.op(e, f_tt(outi, outi, t2, ALU.add), reads=[ko, "cc_t2"], writes=[ko])

    def ssm_main(self):
        A, P, ps = self.A, self.P, self.ps
        identF, onesF = self.identF, self.onesF
        uT = self.uT
        self.after_gy_mark = A.mark()
        Wpads = [A.alloc((8, 2, 512), BF16) for _ in range(2)]
        KP = A.alloc((8, 240), BF16)
        Etab = A.alloc((4, 512))
        T = [A.alloc((512,)) for _ in range(4)]
        mag, phi, sn, cs = T
        Xr = A.alloc((512,)); Xi = A.alloc((512,))
        Hc = A.alloc((2, 4)); Nc = A.alloc((2, 4)); Cc = A.alloc((2, 4)); ct1 = A.alloc((4,)); ct2 = A.alloc((4,))
        dg = A.alloc((4, 128))
        Pbuf = A.alloc((8, 2, 64))
        abb = Pbuf.rearrange("p g c n -> p (g c n)").rearrange("p (a b) -> p a b", b=512)
        PT = A.alloc((8, 128), BF16)
        Ysb = A.alloc((8, 128))
        ytmp = A.alloc((128,))
        sct = [(Xr, "Xr"), (Xi, "Xi"), (Ysb.rearrange("p a b -> p (a b)")[:, 0:512], "Ysb")]
        for gt in range(8):
            def build_wpad(g_):
                wp = Wpads[g_ % 2]
                for s_ in range(8):
                    o = wp[:, s_].rearrange("p c (g n) -> p c g n", n=64)
                    i0 = self.Wc[:, g_, s_].unsqueeze(2).to_broadcast([128, 2, 8, 64])
                    i1 = self.gmask.unsqueeze(1).unsqueeze(3).to_broadcast([128, 2, 8, 64])
                    P.op("pool", f_tt(o, i0, i1, ALU.mult), reads=["Wc", "gmask"], writes=["Wpad%d" % (g_ % 2)])

            if gt == 0:
                build_wpad(0)
            if gt + 1 < 8:
                build_wpad(gt + 1)
            Wpad = Wpads[gt % 2]
            wpk = "Wpad%d" % (gt % 2)
            P.op("pool", f_tt(KP.rearrange("p g (t q) -> p g t q", q=16),
                              self.Kc[:, gt].unsqueeze(1).to_broadcast([128, 8, 15, 16]),
                              self.gmask.unsqueeze(2).unsqueeze(3).to_broadcast([128, 8, 15, 16]), ALU.mult),
                 reads=["Kc", "gmask"], writes=["KP"])
            for a in range(2):
                src = self.abd[a, gt * 8:(gt + 1) * 8, :].rearrange("g n -> (g n)").partition_broadcast(128)
                P.op("sp", f_dma(abb[:, a, :], src), reads=["abd"], writes=["Pbuf"], chan=self.chan("abb"))
            for ti, col in ((0, 42), (1, 43)):
                scol = self.cst[:, col:col + 1]
                P.op("act", f_act(mag, abb[:, 0, :], ACTF.Exp, scale=scol), reads=["Pbuf", "cst"], writes=["T0"])
                P.op("dve", f_ts(phi, abb[:, 1, :], scol, None, ALU.mult), reads=["Pbuf", "cst"], writes=["T1"])
                self.wrap_sincos("dve", phi, sct, "T1", [(sn, "sin", "T2"), (cs, "cos", "T3")])
                P.op("dve", f_tt(Etab[:, 2 * ti, :], mag, cs, ALU.mult), reads=["T0", "T3"], writes=["Etab"])
                P.op("dve", f_tt(Etab[:, 2 * ti + 1, :], mag, sn, ALU.mult), reads=["T0", "T2"], writes=["Etab"])
            P.op("dve", f_memset(Hc, 0.0), writes=["Hc"])
            Emr, Emi, Epr, Epi = Etab[:, 0, :], Etab[:, 1, :], Etab[:, 2, :], Etab[:, 3, :]
            RTg = self.RT[:, gt]

            def lhs(JT, s):
                return uT[:, gt, JT * 1024 + s:JT * 1024 + 1024:8]

            def stageA(JT):
                b0 = 0 if JT % 2 == 0 else 2
                for s in range(8):
                    P.op("pe", f_mm(ps[b0], lhs(JT, s), Wpad[:, s, 0, :], s == 0, s == 7), reads=["uT", wpk], writes=["ps%d" % b0])
                    P.op("pe", f_mm(ps[b0 + 1], lhs(JT, s), Wpad[:, s, 1, :], s == 0, s == 7), reads=["uT", wpk], writes=["ps%d" % (b0 + 1)])

            def stageB(JT):
                b0 = 0 if JT % 2 == 0 else 2
                k0, k1 = "ps%d" % b0, "ps%d" % (b0 + 1)
                P.op("dve", f_tt(T[0], ps[b0], Emr, ALU.mult), reads=[k0, "Etab"], writes=["T0"])
                P.op("dve", f_tt(T[1], ps[b0 + 1], Emi, ALU.mult), reads=[k1, "Etab"], writes=["T1"])
                P.op("dve", f_tt(Xr, T[0], T[1], ALU.subtract), reads=["T0", "T1"], writes=["Xr"])
                P.op("dve", f_tt(T[2], ps[b0], Emi, ALU.mult), reads=[k0, "Etab"], writes=["T2"])
                P.op("dve", f_tt(T[3], ps[b0 + 1], Emr, ALU.mult), reads=[k1, "Etab"], writes=["T3"])
                P.op("dve", f_tt(Xi, T[2], T[3], ALU.add), reads=["T2", "T3"], writes=["Xi"])
                cc0 = (JT % 2) * 8
                for c, (X, xk) in enumerate(((Xr, "Xr"), (Xi, "Xi"))):
                    for q in range(4):
                        col = cc0 + c * 4 + q
                        P.op("pe", f_mm(ps[7][:, col:col + 1], X[:, q * 128:(q + 1) * 128], onesF[:, 0:1], True, True, True),
                             reads=["onesF", xk], writes=["ps7"])

            def stageC(JT):
                full = JT >= 2
                tok0 = JT * 1024
                cc0 = (JT % 2) * 8
                if full:
                    self.cmul_cols(Cc[:, 0, :], Cc[:, 1, :], "Cc", RTg[:, 4, :], RTg[:, 5, :], Hc[:, 0, :], Hc[:, 1, :], ct1, ct2, ["RT", "Hc"])
                    for c, (X, xk, bank) in enumerate(((Xr, "Xr", 4), (Xi, "Xi", 5))):
                        for q in range(4):
                            P.op("dve", f_ts(dg[:, q, :], identF, Cc[:, c, q:q + 1], None, ALU.mult), reads=["identF", "Cc"], writes=["dg"])
                        P.op("pe", f_mm(ps[bank], self.Lstrict, X, True, False, True), reads=["Ls", xk], writes=["ps%d" % bank])
                        for q in range(4):
                            P.op("pe", f_mm(ps[bank][:, q * 128:(q + 1) * 128], onesF, dg[:, q, :], False, True, True),
                                 reads=["onesF", "dg"], writes=["ps%d" % bank])
                self.cmul_cols(Nc[:, 0, :], Nc[:, 1, :], "Nc", ps[7][:, cc0:cc0 + 4], ps[7][:, cc0 + 4:cc0 + 8], RTg[:, 0, :], RTg[:, 1, :], ct1, ct2, ["ps7", "RT"])
                self.cmul_cols(Nc[:, 0, :], Nc[:, 1, :], "Nc", Hc[:, 0, :], Hc[:, 1, :], RTg[:, 2, :], RTg[:, 3, :], ct1, ct2, ["Hc", "RT"], accumulate=True)
                P.op("dve", f_copy(Hc, Nc), reads=["Nc"], writes=["Hc"])
                if not full:
                    return
                Pr = Pbuf[:, :, 0, :]
                Pi = Pbuf[:, :, 1, :]
                v3 = lambda ap: ap.rearrange("p (g n) -> p g n", n=64)
                P.op("dve", f_tt(T[0], ps[4], Epr, ALU.mult), reads=["ps4", "Etab"], writes=["T0"])
                P.op("dve", f_tt(T[1], ps[5], Epi, ALU.mult), reads=["ps5", "Etab"], writes=["T1"])
                P.op("dve", f_tt(Pr, v3(T[0]), v3(T[1]), ALU.subtract), reads=["T0", "T1"], writes=["Pbuf"])
                P.op("dve", f_tt(T[2], ps[4], Epi, ALU.mult), reads=["ps4", "Etab"], writes=["T2"])
                P.op("dve", f_tt(T[3], ps[5], Epr, ALU.mult), reads=["ps5", "Etab"], writes=["T3"])
                P.op("dve", f_tt(Pi, v3(T[2]), v3(T[3]), ALU.add), reads=["T2", "T3", "Pbuf"], writes=["Pbuf"])
                for gl in range(8):
                    bank = 6 + (gl // 4)
                    P.op("pe", f_tr(ps[bank][:, (gl % 4) * 128:(gl % 4 + 1) * 128], Pbuf[:, gl].rearrange("p c n -> p (c n)"), identF),
                         reads=["Pbuf", "identF"], writes=["ps%d" % bank])
                for gl in range(8):
                    bank = 6 + (gl // 4)
                    P.op("act", f_act(PT[:, gl, :], ps[bank][:, (gl % 4) * 128:(gl % 4 + 1) * 128], ACTF.Identity),
                         reads=["ps%d" % bank], writes=["PT"])
                KPv = KP.rearrange("p g (t q) -> p g t q", q=16)
                for s in range(8):
                    for hb in range(2):
                        rhs = KPv[:, 4 * hb:4 * hb + 4, 7 - s:15 - s, :].rearrange("p g t q -> p g (t q)")
                        P.op("pe", f_mm(ps[4 + hb].rearrange("p (g x) -> p g x", x=128), lhs(JT, s), rhs, s == 0, False, True),
                             reads=["uT", "KP"], writes=["ps%d" % (4 + hb)])
                for gl in range(8):
                    bank = 4 + gl // 4
                    g = gt * 8 + gl
                    P.op("pe", f_mm(ps[bank][:, (gl % 4) * 128:(gl % 4 + 1) * 128], PT[:, gl, :],
                                    self.WoutK[:, g, 1:9, :].rearrange("m k p -> m (k p)"), False, True, True),
                         reads=["PT", "WoutK"], writes=["ps%d" % bank])
                for b in range(2):
                    o = Ysb[:, :, b * 64:(b + 1) * 64].rearrange("p t (g q) -> p t g q", q=16)
                    i = ps[4 + b].rearrange("p (g t q) -> p t g q", t=8, q=16)
                    P.op("act" if b == 0 else "dve", f_copy(o, i) if b else f_act(o, i, ACTF.Identity), reads=["ps%d" % (4 + b)], writes=["Ysb"])
                c0 = 112 if JT == 2 else 0
                for tp in range(8):
                    bank = 6 + tp % 2
                    P.op("pe", f_tr(ps[bank][:, 0:128], Ysb[:, tp, :], identF), reads=["Ysb", "identF"], writes=["ps%d" % bank])
                    uin = uT[:, gt, tok0 + 8 * c0 + tp:tok0 + 1024:8]
                    P.op("dve", f_stt(ytmp[:, c0:128], uin, self.Dcol[:, gt:gt + 1], ps[bank][:, c0:128], ALU.mult, ALU.add),
                         reads=["uT", "Dcol", "ps%d" % bank], writes=["ytmp"])
                    if JT == 2:
                        dst = self.GY[:, gt, tp:128:8]
                    else:
                        dst = self.GY[:, gt, 128 + tp:1152:8]
                    P.op("act", f_act(dst, ytmp[:, c0:128], ACTF.Gelu), reads=["ytmp"], writes=["GY"])

            stageA(0); stageA(1); stageB(0); stageA(2); stageB(1); stageC(0)
            stageA(3); stageB(2); stageC(1); stageC(2); stageB(3); stageC(3)
        self.dump("GY", self.GY, [128, 8, 1152], ["GY"])
        P.barrier()
        A.reset(self.after_gy_mark)
        A.free_top(self.uT_bytes)

    def gbc_build(self, dst, c0, vec=None, key="gbc"):
        P, ps = self.P, self.ps
        src = self.MOD[:, c0:c0 + 16] if vec is None else vec
        rk = ["identF", "MOD", "sp"]
        for kt in range(KT):
            b = kt % 2
            P.op("dve", f_ts(self.diag[:, b, :], self.identF, src[:, kt:kt + 1], None, ALU.mult), reads=rk, writes=["diag%d" % b])
            bank = 4 + kt // 4
            P.op("pe", f_mm(ps[bank][:, (kt % 4) * 128:(kt % 4 + 1) * 128], self.onesF, self.diag[:, b, :]),
                 reads=["onesF", "diag%d" % b], writes=["ps%d" % bank])
        for q in range(4):
            P.op("act", f_act(dst[:, q * 512:(q + 1) * 512], ps[4 + q], ACTF.Identity), reads=["ps%d" % (4 + q)], writes=[key])

    def mixer(self):
        A, P, ps = self.A, self.P, self.ps
        identB = self.identB
        P.barrier()
        A.reset(self.res0_mark)
        MIXbuf = A.alloc((16, 1280), BF16)
        MIX = MIXbuf[:, :, 0:1152]
        mix_end = A.mark()
        hT = A.alloc((16, 1280), BF16)
        qT = A.alloc((8, 1152), BF16)
        kd = A.alloc((2, 1280), BF16)
        vpad = A.alloc((4, 10, 128), BF16)
        ph_mark = A.mark()
        s1bc = A.alloc((D,), BF16); sh1bc = A.alloc((D,), BF16)
        self.gbc_build(s1bc, 0, vec=self.sp[:, 0, :], key="s1bc")
        self.gbc_build(sh1bc, 0, key="sh1bc")
        self.norm_bufs(True)
        xt = [A.alloc((D,)) for _ in range(2)]
        for w in range(22, 32):
            xs = w % 2
            P.op("sp", f_dma(xt[xs], self.xw[w * 128:(w + 1) * 128, :]), writes=["xt%d" % xs], chan=self.chan("xt%d" % xs))
            self.norm_to_fm(xt[xs], "xt%d" % xs, lambda half, w=w: hT[:, half * 8:(half + 1) * 8, (w - 22) * 128:(w - 21) * 128], "hT",
                            s1bc, "s1bc", sh1bc, "sh1bc")
        P.barrier()
        A.reset(ph_mark)
        NS = 4
        wsl = [A.alloc((16, 128), BF16) for _ in range(NS)]
        vT = A.alloc((1280,), BF16)
        vtm = A.alloc((128,), BF16)
        win = self.w_in.rearrange("(kt p) m -> p kt m", p=128)
        jobs = [("q", i, i * 128) for i in range(8)] + [("k", 0, 1024), ("k", 1, 1088), ("v", 0, 1152)]

        def wload(ji):
            kind, idx, c0 = jobs[ji]
            s = ji % NS
            if kind == "k":
                P.op("pool", f_dma(wsl[s][:, :, 0:64], win[:, :, c0:c0 + 64]), writes=["wsl%d" % s], chan=self.chan("wsl%d" % s))
                P.op("pool", f_dma(wsl[s][:, :, 64:128], win[:, :, c0:c0 + 64]), writes=["wsl%d" % s], chan=self.chan("wsl%d" % s))
            else:
                P.op("pool", f_dma(wsl[s], win[:, :, c0:c0 + 128]), writes=["wsl%d" % s], chan=self.chan("wsl%d" % s))

        for ji in range(min(NS, len(jobs))):
            wload(ji)
        ev = 0
        for ji, (kind, idx, c0) in enumerate(jobs):
            s = ji % NS
            chunks = [(128, 640), (640, 1152), (1152, 1280)] if kind == "q" else [(0, 512), (512, 1024), (1024, 1280)]
            for ci, (a, b) in enumerate(chunks):
                bank = ev % 4
                for kt in range(KT):
                    P.op("pe", f_mm(ps[bank][:, 0:b - a], wsl[s][:, kt, :], hT[:, kt, a:b], kt == 0, kt == KT - 1),
                         reads=["wsl%d" % s, "hT"], writes=["ps%d" % bank])
                if kind == "q":
                    dst, dk = qT[:, idx, a - 128:b - 128], "q%d" % idx
                elif kind == "k":
                    dst, dk = kd[:, idx, a:b], "kd"
                else:
                    dst, dk = vT[:, a:b], "vT"
                if ev % 2 == 0:
                    P.op("act", f_act(dst, ps[bank][:, 0:b - a], ACTF.Identity), reads=["ps%d" % bank], writes=[dk])
                else:
                    P.op("dve", f_copy(dst, ps[bank][:, 0:b - a]), reads=["ps%d" % bank], writes=[dk])
                ev += 1
            if ji + NS < len(jobs):
                wload(ji + NS)
        P.op("dve", f_memset(vpad, 0.0), writes=["vpad"])
        psb = ps[6].bitcast(BF16)
        for kc in range(10):
            P.op("pe", f_tr(psb[:, 0:128], vT[:, kc * 128:(kc + 1) * 128], identB), reads=["vT", "identB"], writes=["ps6"])
            P.op("act", f_act(vtm, psb[:, 0:128], ACTF.Identity), reads=["ps6"], writes=["vtm"])
            for h in range(2):
                for e in range(2):
                    P.op("dve", f_copy(vpad[:, h * 2 + e, kc, 64 * e:64 * e + 64], vtm[:, 64 * h:64 * h + 64]), reads=["vtm", "vpad"], writes=["vpad"])
        self.dump("qT", qT, [128, 8, 1152], ["q%d" % i for i in range(8)])
        self.dump("kd", kd, [128, 2, 1280], ["kd"])
        srow = A.alloc((16,))
        es2 = A.alloc((8,))
        P.op("sp", f_dma(srow[0:1], self.sinks.rearrange("(o a) -> o a", o=1)), writes=["srow"], chan=self.chan("srow"))
        P.op("act", f_act(srow[0:1], srow[0:1], ACTF.Exp), reads=["srow"], writes=["srow"])
        P.op("pe", f_mm(ps[7][:, 0:16], self.onesF[0:1, :], srow[0:1]), reads=["onesF", "srow"], writes=["ps7"])
        P.op("dve", f_copy(es2[0:64], ps[7][0:64, 0:16:2]), reads=["ps7"], writes=["es2"])
        P.op("dve", f_copy(es2[64:128], ps[7][64:128, 1:16:2]), reads=["ps7", "es2"], writes=["es2"])
        Pb = [A.alloc((2, 10, 256), BF16) for _ in range(2)]
        rr = A.alloc((128,))
        sc = 0
        for i in range(8):
            h = i // 4
            pb = Pb[i % 2]
            pk = "Pb%d" % (i % 2)
            for kc in range(10):
                kb = kc - 1
                if kb == -1:
                    qa, qb_, m0, m1, o0 = 0, 128, 128, 256, 128
                elif kb == 8:
                    qa, qb_, m0, m1, o0 = 1024, 1152, 0, 128, 0
                else:
                    qa, qb_, m0, m1, o0 = kb * 128, kb * 128 + 256, 0, 256, 0
                N = qb_ - qa
                for e in range(2):
                    bank = sc % 2
                    sc += 1
                    P.op("pe", f_mm(ps[bank][:, 0:N], kd[64 * e:64 * e + 64, h, kc * 128:(kc + 1) * 128], qT[64 * e:64 * e + 64, i, qa:qb_], True, True),
                         reads=["kd", "q%d" % i], writes=["ps%d" % bank])
                    pk2 = pk + "_%d_%d" % (e, kc % 2)
                    P.op("act", f_act(pb[:, e, kc, o0:o0 + N], ps[bank][:, 0:N], ACTF.Exp, bias=self.cst[:, 32 + kc:33 + kc], scale=0.125),
                         reads=["ps%d" % bank, "cst"], writes=[pk2])
                    P.op("dve", f_tt(pb[:, e, kc, o0:o0 + N], pb[:, e, kc, o0:o0 + N], self.mask01[:, m0:m1], ALU.mult),
                         reads=[pk2, "mask01"], writes=[pk2])
            for n in range(9):
                bo, bd = 2 + n % 2, 4 + n % 2
                terms = []
                for e in range(2):
                    terms.append((e, n, slice(128, 256)))
                    terms.append((e, n + 1, slice(0, 128)))
                for ti, (e, kc, sl) in enumerate(terms):
                    P.op("pe", f_mm(ps[bo][:, 0:128], vpad[:, h * 2 + e, kc, :], pb[:, e, kc, sl], ti == 0, ti == 3), reads=["vpad", pk + "_%d_%d" % (e, kc % 2)], writes=["ps%d" % bo])
                for ti, (e, kc, sl) in enumerate(terms):
                    P.op("pe", f_mm(ps[bd][:, 0:128], self.onespad[:, e, :], pb[:, e, kc, sl], ti == 0, ti == 3), reads=["onespad", pk + "_%d_%d" % (e, kc % 2)], writes=["ps%d" % bd])
                P.op("dve", f_ts(rr, ps[bd][:, 0:128], es2[:, i:i + 1], None, ALU.add), reads=["ps%d" % bd, "es2"], writes=["rr"])
                P.op("dve", f_recip(rr, rr), reads=["rr"], writes=["rr"])
                P.op("dve", f_tt(qT[:, i, n * 128:(n + 1) * 128], ps[bo][:, 0:128], rr, ALU.mult), reads=["ps%d" % bo, "rr"], writes=["q%d" % i])
        self.dump("attn", qT, [128, 8, 1152], ["q%d" % i for i in range(8)])
        P.barrier()
        A.reset(ph_mark)
        GYb = self.GY
        wg = [A.alloc((56, 128), BF16) for _ in range(2)]
        tA = A.alloc((512,), BF16); tS = A.alloc((512,), BF16); tB = A.alloc((512,)); tT = A.alloc((512,))
        wap = self.w_ap.rearrange("(kt p) m -> p kt m", p=128)
        wgl = self.w_glu.rearrange("(kt p) m -> p kt m", p=128)

        def gload(mt):
            s = mt % 2
            k, c = "wg%d" % s, self.chan("wg%d" % s)
            P.op("pool", f_dma(wg[s][:, 0:16, :], win[:, :, 2304 + mt * 128:2304 + (mt + 1) * 128]), writes=[k], chan=c)
            P.op("pool", f_dma(wg[s][:, 16:32, :], win[:, :, 4352 + mt * 128:4352 + (mt + 1) * 128]), writes=[k], chan=c)
            P.op("pool", f_dma(wg[s][:, 32:40, :], wap[:, :, mt * 128:(mt + 1) * 128]), writes=[k], chan=c)
            P.op("pool", f_dma(wg[s][:, 40:48, :], wgl[:, :, mt * 128:(mt + 1) * 128]), writes=[k], chan=c)
            P.op("pool", f_dma(wg[s][:, 48:56, :], wgl[:, :, D + mt * 128:D + (mt + 1) * 128]), writes=[k], chan=c)

        gload(0)
        gload(1)
        for mt in range(16):
            s = mt % 2
            k = "wg%d" % s
            for (a, b) in [(0, 512), (512, 1024), (1024, 1152)]:
                n = b - a
                for kt in range(16):
                    P.op("pe", f_mm(ps[0][:, 0:n], wg[s][:, kt, :], hT[:, kt, 128 + a:128 + b], kt == 0, kt == 15), reads=[k, "hT"], writes=["ps0"])
                P.op("act", f_act(tA[:, 0:n], ps[0][:, 0:n], ACTF.Sigmoid), reads=["ps0"], writes=["tA"])
                for kt in range(8):
                    P.op("pe", f_mm(ps[1][:, 0:n], wg[s][:, 32 + kt, :], qT[:, kt, a:b], kt == 0, kt == 7), reads=[k] + ["q%d" % kt], writes=["ps1"])
                P.op("dve", f_tt(MIX[:, mt, a:b], ps[1][:, 0:n], tA[:, 0:n], ALU.mult), reads=["ps1", "tA"], writes=["MIX"])
                for kt in range(16):
                    P.op("pe", f_mm(ps[2][:, 0:n], wg[s][:, 16 + kt, :], hT[:, kt, 128 + a:128 + b], kt == 0, kt == 15), reads=[k, "hT"], writes=["ps2"])
                P.op("act", f_act(tS[:, 0:n], ps[2][:, 0:n], ACTF.Sigmoid), reads=["ps2"], writes=["tS"])
                for kt in range(8):
                    P.op("pe", f_mm(ps[3][:, 0:n], wg[s][:, 40 + kt, :], GYb[:, kt, a:b], kt == 0, kt == 7), reads=[k, "GY"], writes=["ps3"])
                for kt in range(8):
                    P.op("pe", f_mm(ps[4][:, 0:n], wg[s][:, 48 + kt, :], GYb[:, kt, a:b], kt == 0, kt == 7), reads=[k, "GY"], writes=["ps4"])
                P.op("act", f_act(tB[:, 0:n], ps[4][:, 0:n], ACTF.Sigmoid), reads=["ps4"], writes=["tB"])
                P.op("dve", f_tt(tT[:, 0:n], ps[3][:, 0:n], tB[:, 0:n], ALU.mult), reads=["ps3", "tB"], writes=["tT"])
                P.op("dve", f_tt(tT[:, 0:n], tT[:, 0:n], tS[:, 0:n], ALU.mult), reads=["tT", "tS"], writes=["tT"])
                P.op("dve", f_tt(MIX[:, mt, a:b], MIX[:, mt, a:b], tT[:, 0:n], ALU.add), reads=["MIX", "tT"], writes=["MIX"])
            if mt + 2 < 16:
                gload(mt + 2)
        self.dump("MIX", MIX, [128, 16, 1152], ["MIX"])
        P.barrier()
        A.reset(mix_end)
        A.free_top(self.GY_bytes)
        self.x1, self.x1_bytes = A.alloc_top((9, D))
        x1 = self.x1
        g1bc = A.alloc((D,))
        self.gbc_build(g1bc, 32)
        Wb = [A.alloc((16, 512), BF16) for _ in range(2)]
        xc = [A.alloc((512,)) for _ in range(2)]
        wo = self.w_out.rearrange("(kt p) m -> p kt m", p=128)
        P.op("pool", f_dma(Wb[0], wo[:, :, 0:512]), writes=["Wb0"], chan=self.chan("Wb0"))
        cnt = 0
        for cb in range(4):
            s = cb % 2
            if cb + 1 < 4:
                P.op("pool", f_dma(Wb[1 - s], wo[:, :, (cb + 1) * 512:(cb + 2) * 512]), writes=["Wb%d" % (1 - s)], chan=self.chan("Wb%d" % (1 - s)))
            for n in range(9):
                bank = cnt % 4
                xs = cnt % 2
                cnt += 1
                P.op("sp", f_dma(xc[xs], self.xw[(23 + n) * 128:(24 + n) * 128, cb * 512:(cb + 1) * 512]), writes=["xc%d" % xs], chan=self.chan("xc%d" % xs))
                for kt in range(16):
                    P.op("pe", f_mm(ps[bank], MIX[:, kt, n * 128:(n + 1) * 128], Wb[s][:, kt, :], kt == 0, kt == 15), reads=["MIX", "Wb%d" % s], writes=["ps%d" % bank])
                dst = x1[:, n, cb * 512:(cb + 1) * 512]
                P.op("dve", f_tt(dst, ps[bank], g1bc[:, cb * 512:(cb + 1) * 512], ALU.mult), reads=["ps%d" % bank, "gbc"], writes=["x1_%d" % n])
                P.op("dve", f_tt(dst, dst, xc[xs], ALU.add), reads=["x1_%d" % n, "xc%d" % xs], writes=["x1_%d" % n])
        self.dump("x1", x1, [128, 9, D], ["x1_%d" % n for n in range(9)])
        P.barrier()
        A.reset(mix_end)
        self.h2T = MIXbuf[:, :, 0:1152]
        s2bc = A.alloc((D,), BF16); sh2bc = A.alloc((D,), BF16)
        self.gbc_build(s2bc, 0, vec=self.sp[:, 1, :], key="s2bc")
        self.gbc_build(sh2bc, 48, key="sh2bc")
        self.norm_bufs(True)
        for n in range(9):
            self.norm_to_fm(x1[:, n, :], "x1_%d" % n, lambda half, n=n: self.h2T[:, half * 8:(half + 1) * 8, n * 128:(n + 1) * 128], "h2T",
                            s2bc, "s2bc", sh2bc, "sh2bc")
        P.barrier()
        A.reset(mix_end)

    def ffn(self):
        A, P, ps = self.A, self.P, self.ps
        identF = self.identF
        x1, h2T = self.x1, self.h2T
        cw = A.alloc((132,)); cbias = A.alloc((44,))
        cwd = self.conv_w.rearrange("k (m p) -> (k m) p", p=128)
        rows = self.rows
        for (r0, r1) in ((0, 128), (128, 132)):
            nr = r1 - r0
            P.op("sp", f_dma(rows[0:nr, :], cwd[r0:r1, :]), writes=["rows"], chan=self.chan("rows"))
            P.op("pe", f_tr(ps[0][:, 0:nr], rows[0:nr, :], identF[0:nr, 0:nr]), reads=["rows", "identF"], writes=["ps0"])
            P.op("dve", f_copy(cw[:, r0:r1], ps[0][:, 0:nr]), reads=["ps0"], writes=["cw"])
        P.op("sp", f_dma(rows[0:44, :], self.conv_b.rearrange("(m p) -> m p", p=128)), writes=["rows"], chan=self.chan("rows"))
        P.op("pe", f_tr(ps[0][:, 0:44], rows[0:44, :], identF[0:44, 0:44]), reads=["rows", "identF"], writes=["ps0"])
        P.op("dve", f_copy(cbias, ps[0][:, 0:44]), reads=["ps0"], writes=["cbias"])
        g2bc = A.alloc((D,))
        self.gbc_build(g2bc, 80)
        ACTB = A.alloc((11, 1024), BF16)
        GP = [A.alloc((1032,))] * 2
        cA = [A.alloc((1024,))] * 2
        sB = [A.alloc((1024,), BF16)] * 2
        wu = [A.alloc((2, 16, 128), BF16) for _ in range(2)]
        wd = [A.alloc((11, 512), BF16) for _ in range(2)]
        tmp = [A.alloc((512,)) for _ in range(2)]
        wup = self.w_up.rearrange("(kt p) m -> p kt m", p=128)
        wdn = self.w_down.rearrange("(kt p) m -> p kt m", p=128)

        def uload(mt):
            s = mt % 2
            P.op("pool", f_dma(wu[s][:, 0], wup[:, :, mt * 128:(mt + 1) * 128]), writes=["wu%d" % s], chan=self.chan("wu%d" % s))
            P.op("pool", f_dma(wu[s][:, 1], wup[:, :, 5632 + mt * 128:5632 + (mt + 1) * 128]), writes=["wu%d" % s], chan=self.chan("wu%d" % s))

        def dload(ch, cb, s):
            P.op("pool", f_dma(wd[s], wdn[:, ch * 11:(ch + 1) * 11, cb * 512:(cb + 1) * 512]), writes=["wd%d" % s], chan=self.chan("wd%d" % s))

        uload(0)
        uload(1)
        dcnt = 0
        for ch in range(4):
            for pr in range(11):
                mt = ch * 11 + pr
                s = mt % 2
                k = "wu%d" % s
                gp, ca, sb = GP[s], cA[s], sB[s]
                for kt in range(16):
                    P.op("pe", f_mm(ps[0], wu[s][:, 0, kt, :], h2T[:, kt, 128:640], kt == 0, kt == 15), reads=[k, "h2T"], writes=["ps0"])
                for kt in range(16):
                    P.op("pe", f_mm(ps[1], wu[s][:, 0, kt, :], h2T[:, kt, 640:1152], kt == 0, kt == 15), reads=[k, "h2T"], writes=["ps1"])
                for kt in range(16):
                    P.op("pe", f_mm(ps[2][:, 0:8], wu[s][:, 0, kt, :], h2T[:, kt, 120:128], kt == 0, kt == 15), reads=[k, "h2T"], writes=["ps2"])
                for kt in range(16):
                    P.op("pe", f_mm(ps[3], wu[s][:, 1, kt, :], h2T[:, kt, 128:640], kt == 0, kt == 15), reads=[k, "h2T"], writes=["ps3"])
                for kt in range(16):
                    P.op("pe", f_mm(ps[4], wu[s][:, 1, kt, :], h2T[:, kt, 640:1152], kt == 0, kt == 15), reads=[k, "h2T"], writes=["ps4"])
                gk = "GP0"
                P.op("dve", f_ts(gp[:, 0:8], ps[2][:, 0:8], self.cst[:, 23:24], None, ALU.mult), reads=["ps2", "cst"], writes=[gk])
                P.op("act", f_act(gp[:, 8:520], ps[0], ACTF.Identity), reads=["ps0", gk], writes=[gk])
                P.op("act", f_act(gp[:, 520:1032], ps[1], ACTF.Identity), reads=["ps1", gk], writes=[gk])
                w0, w1, w2 = cw[:, mt:mt + 1], cw[:, 44 + mt:45 + mt], cw[:, 88 + mt:89 + mt]
                ck = "cA0"
                P.op("dve", f_ts(ca, gp[:, 6:1030], w0, cbias[:, mt:mt + 1], ALU.mult, ALU.add), reads=[gk, "cw", "cbias"], writes=[ck])
                P.op("dve", f_stt(ca, gp[:, 7:1031], w1, ca, ALU.mult, ALU.add), reads=[gk, "cw", ck], writes=[ck])
                P.op("dve", f_stt(ca, gp[:, 8:1032], w2, ca, ALU.mult, ALU.add), reads=[gk, "cw", ck], writes=[ck])
                P.op("act", f_act(sb, ca, ACTF.Silu), reads=[ck], writes=["sB0"])
                P.op("dve", f_tt(ACTB[:, pr, 0:512], ps[3], sb[:, 0:512], ALU.mult), reads=["ps3", "sB0"], writes=["ACTB"])
                P.op("dve", f_tt(ACTB[:, pr, 512:1024], ps[4], sb[:, 512:1024], ALU.mult), reads=["ps4", "sB0"], writes=["ACTB"])
                if mt + 2 < 44:
                    uload(mt + 2)
            dload(ch, 0, dcnt % 2)
            for cb in range(4):
                s = dcnt % 2
                dcnt += 1
                if cb + 1 < 4:
                    dload(ch, cb + 1, 1 - s)
                for n in range(1, 9):
                    bank = 5 + n % 3
                    ts_ = n % 2
                    for kt in range(11):
                        P.op("pe", f_mm(ps[bank], ACTB[:, kt, (n - 1) * 128:n * 128], wd[s][:, kt, :], kt == 0, kt == 10),
                             reads=["ACTB", "wd%d" % s], writes=["ps%d" % bank])
                    dst = x1[:, n, cb * 512:(cb + 1) * 512]
                    P.op("dve", f_tt(tmp[ts_], ps[bank], g2bc[:, cb * 512:(cb + 1) * 512], ALU.mult), reads=["ps%d" % bank, "gbc"], writes=["tmp%d" % ts_])
                    P.op("dve", f_tt(dst, dst, tmp[ts_], ALU.add), reads=["x1_%d" % n, "tmp%d" % ts_], writes=["x1_%d" % n])
        self.dump("x2", x1, [128, 9, D], ["x1_%d" % n for n in range(9)])
        fgbc = g2bc
        P.op("sp", f_dma(fgbc, self.final_g.partition_broadcast(128)), reads=["gbc"], writes=["gbc"], chan=self.chan("fgbc"))
        junk = cA[0].bitcast(BF16)
        st_ = A.alloc((4,))
        for n in range(1, 9):
            i = n % 2
            ss, rs = st_[:, 2 * i:2 * i + 1], st_[:, 2 * i + 1:2 * i + 2]
            xk = "x1_%d" % n
            P.op("act", f_act(junk, x1[:, n, :], ACTF.Square, accum_out=ss), reads=[xk], writes=["cA0", "fss%d" % i])
            P.op("act", f_act(rs, ss, ACTF.Sqrt, bias=1e-6, scale=1.0 / D), reads=["fss%d" % i], writes=["frs%d" % i])
            P.op("dve", f_recip(rs, rs), reads=["frs%d" % i], writes=["frs%d" % i])
            P.op("dve", f_stt(x1[:, n, :], x1[:, n, :], rs, fgbc, ALU.mult, ALU.mult), reads=[xk, "frs%d" % i, "gbc"], writes=[xk])
            P.op("sp", f_dma(self.out[(n - 1) * 128:n * 128, :], x1[:, n, :]), reads=[xk], chan=self.chan("out"))

    def finish(self):
        self.P.emit(self.nc, self.st)
        self.st.close()
        return self.nc


def build(debug=(), stop_after=None):
    B = Builder(debug, stop_after)
    B.setup()
    gen = B.ssm_precompute()
    next(gen)
    for blk in range(24):
        B.mod_tiles(blk * 4, blk * 4 + 4)
        if blk >= 2:
            next(gen, None)
    for _ in gen:
        pass
    stages = [("setup_rest", "setup_rest"), ("setup_b", "setup_b"), ("phase_u", "phase_u"), ("ssm", "ssm_main"),
              ("mixer", "mixer"), ("ffn", "ffn")]
    for name, meth in stages:
        getattr(B, meth)()
        if stop_after == name:
            break
    nc = B.finish()
    return B, nc


def host_inputs(inp):
    f = lambda a: np.ascontiguousarray(np.asarray(a, dtype=np.float32))
    x = f(inp["x"])
    shared = {
        "ada_w": f(inp["ada_w"][0]), "ada_b": f(inp["ada_b"][0]), "g_mix": f(inp["norm_mix_g"][0]),
        "w_in": f(inp["w_in"][0]), "sinks": f(inp["attn_sinks"][0]), "w_ap": f(inp["w_attn_proj"][0]),
        "a_re": f(inp["ssm_a_re"][0]), "a_im": f(inp["ssm_a_im"][0]), "log_dt": f(inp["ssm_log_dt"][0]),
        "b_re": f(inp["ssm_b_re"][0]), "b_im": f(inp["ssm_b_im"][0]), "c_re": f(inp["ssm_c_re"][0]),
        "c_im": f(inp["ssm_c_im"][0]), "ssm_d": f(inp["ssm_d"][0]), "w_glu": f(inp["w_ssm_glu"][0]),
        "w_out": f(inp["w_out"][0]), "g_ffn": f(inp["norm_ffn_g"][0]), "w_up": f(inp["w_ffn_up"][0]),
        "conv_w": f(inp["ffn_conv_w"][0]), "conv_b": f(inp["ffn_conv_b"][0]), "w_down": f(inp["w_ffn_down"][0]),
        "final_g": f(inp["final_g"]),
    }
    maps = []
    p = np.arange(128, dtype=np.float32)
    for c in range(8):
        b, j = c // 4, c % 4
        start = 1024 * (j + 1) - 4096
        xw = np.zeros((4096, D), np.float32)
        lo = max(0, -start)
        xw[lo:] = x[b, start + lo:start + 4096]
        cst = np.zeros((128, 64), np.float32)
        for w in range(NW):
            cst[:, w] = 1.0 if start + 128 * w >= 0 else 0.0
        for e in range(10):
            cst[:, 32 + e] = 0.0 if start + 128 * (22 + e) >= 0 else NEG
        cst[:, 42] = 64.0 - p
        cst[:, 43] = p - 65.0
        cst[:, 44] = np.where(p < 64, 1.0, -1.0)
        m = dict(shared)
        m["xw"] = xw
        m["cb"] = f(inp["c"][b])
        m["cst"] = cst
        maps.append(m)
    return maps


def run_cores(inp, debug=(), stop_after=None, trace=False):
    B, nc = build(debug, stop_after)
    maps = host_inputs(inp)
    maps = [{k: m[k] for k in B.in_names} for m in maps]
    res = run_bass_kernel_spmd(nc, maps, core_ids=list(range(8)))
    return B, res


def kernel(**inputs):
    B, res = run_cores(inputs)
    out = np.zeros((2, 4096, D), np.float32)
    for c in range(8):
        b, j = c // 4, c % 4
        out[b, 1024 * j:1024 * (j + 1)] = np.asarray(res.results[c]["out"], dtype=np.float32)
    return out
```
